# Optimizing a Trainium2 kernel written in Bass

```python
import math
import jax
import jax.numpy as jnp
from jax import lax
import numpy as np

D_MODEL = 1024
BATCH = 8
SEQ = 4096
DEPTH = 2

CTX_LEN = 256
GRID_W = 64
N_MIXERS = 4
GROUP_W = D_MODEL // N_MIXERS
HEADS = 4
HEAD_V = GROUP_W // HEADS
HEAD_QK = HEAD_V // 2
N_DIR = 2
D_FF = 4 * D_MODEL
N_MOD = 6
CHUNK = 64
EPS = 1e-6
GN_EPS = 64e-5
M_INIT = -1e30
ML_SIZES = (HEADS * HEAD_QK, HEADS * HEAD_QK, GROUP_W, GROUP_W, N_DIR * HEADS, N_DIR * HEADS)
RW_DECAY_LORA = 64
RW_ICLR_LORA = 64
RW_GATE_LORA = 128
RW_DECAY_SCALE = math.exp(-0.5)
RW_SIZES = (GROUP_W, GROUP_W, GROUP_W, RW_DECAY_LORA, RW_ICLR_LORA, RW_GATE_LORA)
GLA_GATE_LORA = 16
GLA_LOGIT_NORM = 16.0
GL_SIZES = (HEADS * HEAD_QK, HEADS * HEAD_QK, GROUP_W, GLA_GATE_LORA, GROUP_W)
GD_CONV = 3
GD_SIZES = (3 * GROUP_W, GROUP_W, N_DIR * HEADS, N_DIR * HEADS)
ML_W = sum(ML_SIZES)
RW_W = sum(RW_SIZES)
GL_W = sum(GL_SIZES)
GD_W = sum(GD_SIZES)
P_IN = ML_W + RW_W + GL_W + GD_W

kernel_name = 'hybrid_mlstm_rwkv7_gla_gdn_prefix_block'


def _split(z, sizes):
    idx, acc = [], 0
    for s in sizes[:-1]:
        acc += s
        idx.append(acc)
    return jnp.split(z, idx, axis=-1)


def _rms(x, gain):
    xf = x.astype(jnp.float32)
    y = xf * lax.rsqrt(jnp.mean(jnp.square(xf), -1, keepdims=True) + EPS)
    return (y * gain.astype(jnp.float32)).astype(x.dtype)


def _head_rms(h, gain):
    y = h * lax.rsqrt(jnp.mean(jnp.square(h), -1, keepdims=True) + EPS)
    return y.reshape(h.shape[0], h.shape[1], -1) * gain


def _head_groupnorm(h, gain, bias):
    d = h - jnp.mean(h, -1, keepdims=True)
    y = d * lax.rsqrt(jnp.mean(jnp.square(d), -1, keepdims=True) + GN_EPS)
    return y.reshape(h.shape[0], h.shape[1], -1) * gain + bias


def _l2n(t):
    return t * lax.rsqrt(jnp.sum(jnp.square(t), -1, keepdims=True) + EPS)


def _mlp(h, w1, w2):
    return jnp.square(jax.nn.relu(h @ w1)) @ w2


def _tri(k):
    return jnp.tril(jnp.ones((CHUNK, CHUNK), dtype=bool), k)


def _chunk(t):
    b, l, h, d = t.shape
    return t.reshape(b, l // CHUNK, CHUNK, h, d).transpose(0, 3, 1, 2, 4)


def _unchunk(t):
    b, h, n, c, d = t.shape
    return t.transpose(0, 2, 3, 1, 4).reshape(b, n * c, h, d)


def _to_n(t):
    return jnp.moveaxis(t, 2, 0)


def _from_n(t):
    return jnp.moveaxis(t, 0, 2)


def _rev(t, n_ctx):
    return jnp.concatenate([jnp.flip(t[:, :n_ctx], 1), jnp.flip(t[:, n_ctx:], 1)], axis=1)


def _bidir(scan_fn, n_ctx, fwd_args, bwd_args):
    y_fwd = scan_fn(*fwd_args)
    y_bwd = scan_fn(*[_rev(t, n_ctx) for t in bwd_args])
    return y_fwd + _rev(y_bwd, n_ctx)


def _token_shift(z, n_ctx):
    L = z.shape[1]
    pos = jnp.arange(L)
    zp = jnp.pad(z, ((0, 0), (1, 0), (0, 0)))[:, :-1]
    zn = jnp.pad(z, ((0, 0), (0, 1), (0, 0)))[:, 1:]
    keep_p = ((pos != 0) & (pos != n_ctx))[None, :, None]
    keep_n = ((pos != L - 1) & (pos != n_ctx - 1))[None, :, None]
    return jnp.where(keep_p, zp, 0), jnp.where(keep_n, zn, 0)


def _grid_conv(t, width, w):
    b, l, ch = t.shape
    img = t.reshape(b, l // width, width, ch)
    out = lax.conv_general_dilated(img, w[:, :, None, :].astype(t.dtype), window_strides=(1, 1), padding='SAME',
                                   dimension_numbers=('NHWC', 'HWIO', 'NHWC'), feature_group_count=ch)
    return out.reshape(b, l, ch)


def _mlstm_scan(q, k, v, ig, fg):
    bsz, _, nh, dk = q.shape
    dv = v.shape[-1]
    q = _chunk(q) * dk ** -0.5
    k, v = _chunk(k), _chunk(v)
    ig = _chunk(ig[..., None])[..., 0]
    b = jnp.cumsum(jax.nn.log_sigmoid(_chunk(fg[..., None])[..., 0]), axis=-1)
    b_end = b[..., -1]
    w_end = b_end[..., None] - b + ig
    m_loc = jnp.max(w_end, -1)
    e_end = jnp.exp(w_end - m_loc[..., None])
    c_loc = jnp.einsum('bhnc,bhnck,bhncv->bhnkv', e_end, k, v)
    n_loc = jnp.einsum('bhnc,bhnck->bhnk', e_end, k)

    def step(carry, xs):
        c_s, n_s, m_s = carry
        c_l, n_l, m_l, bl = xs
        m_new = jnp.maximum(bl + m_s, m_l)
        f_s = jnp.exp(bl + m_s - m_new)
        f_l = jnp.exp(m_l - m_new)
        c_new = f_s[..., None, None] * c_s + f_l[..., None, None] * c_l
        n_new = f_s[..., None] * n_s + f_l[..., None] * n_l
        return (c_new, n_new, m_new), (c_s, n_s, m_s)

    init = (jnp.zeros((bsz, nh, dk, dv), jnp.float32), jnp.zeros((bsz, nh, dk), jnp.float32),
            jnp.full((bsz, nh), M_INIT, jnp.float32))
    _, (c0, n0, m0) = lax.scan(step, init, (_to_n(c_loc), _to_n(n_loc), _to_n(m_loc), _to_n(b_end)))
    c0, n0, m0 = _from_n(c0), _from_n(n0), _from_n(m0)
    log_d = jnp.where(_tri(0), b[..., :, None] - b[..., None, :] + ig[..., None, :], -jnp.inf)
    m_prev = b + m0[..., None]
    m_i = jnp.maximum(m_prev, jnp.max(log_d, -1))
    s = jnp.einsum('bhnik,bhnjk->bhnij', q, k) * jnp.exp(log_d - m_i[..., None])
    e_prev = jnp.exp(m_prev - m_i)
    num = jnp.einsum('bhnij,bhnjv->bhniv', s, v) + e_prev[..., None] * jnp.einsum('bhnik,bhnkv->bhniv', q, c0)
    den = jnp.sum(s, -1) + e_prev * jnp.einsum('bhnik,bhnk->bhni', q, n0)
    h = num / jnp.maximum(jnp.abs(den), jnp.exp(-m_i))[..., None]
    return _unchunk(h)


def _rwkv7_scan(r, log_w, k, v, a, b):
    bsz, _, nh, dk = r.shape
    dv = v.shape[-1]
    r, log_w, k, v, a, b = (_chunk(t) for t in (r, log_w, k, v, a, b))
    g = jnp.cumsum(log_w, axis=3)
    g_end = g[..., -1:, :]
    r_h = r * jnp.exp(g)
    a_h = a * jnp.exp(-g)
    k_h = k * jnp.exp(-g)
    b_h = b * jnp.exp(g - log_w)
    m_ab = jnp.where(_tri(-1), jnp.einsum('bhnid,bhnjd->bhnij', b_h, a_h), 0.0)
    m_bk = jnp.where(_tri(-1), jnp.einsum('bhnid,bhnjd->bhnij', b_h, k_h), 0.0)
    rhs = jnp.concatenate([b_h, jnp.einsum('bhnij,bhnjv->bhniv', m_bk, v)], -1)
    sol = lax.linalg.triangular_solve(-m_ab, rhs, left_side=True, lower=True, unit_diagonal=True)
    w_mat, u0 = sol[..., :dk], sol[..., dk:]
    a_end = a * jnp.exp(g_end - g)
    k_end = k * jnp.exp(g_end - g)

    def step(s, xs):
        w_c, u_c, a_c, k_c, v_c, ge = xs
        u_t = u_c + jnp.einsum('bhck,bhkv->bhcv', w_c, s)
        s_new = (jnp.exp(ge)[..., None] * s + jnp.einsum('bhck,bhcv->bhkv', a_c, u_t)
                 + jnp.einsum('bhck,bhcv->bhkv', k_c, v_c))
        return s_new, s

    xs = tuple(_to_n(t) for t in (w_mat, u0, a_end, k_end, v, g_end[..., 0, :]))
    _, s0 = lax.scan(step, jnp.zeros((bsz, nh, dk, dv), jnp.float32), xs)
    s0 = _from_n(s0)
    u = u0 + jnp.einsum('bhnck,bhnkv->bhncv', w_mat, s0)
    att_a = jnp.where(_tri(0), jnp.einsum('bhnid,bhnjd->bhnij', r_h, a_h), 0.0)
    att_k = jnp.where(_tri(0), jnp.einsum('bhnid,bhnjd->bhnij', r_h, k_h), 0.0)
    y = (jnp.einsum('bhnik,bhnkv->bhniv', r_h, s0) + jnp.einsum('bhnij,bhnjv->bhniv', att_a, u)
         + jnp.einsum('bhnij,bhnjv->bhniv', att_k, v))
    return _unchunk(y)


def _gla_scan(q, k, v, log_a):
    bsz, _, nh, dk = q.shape
    dv = v.shape[-1]
    q = _chunk(q) * dk ** -0.5
    k, v = _chunk(k), _chunk(v)
    g = jnp.cumsum(_chunk(log_a), axis=3)
    g_end = g[..., -1, :]
    g_mid = g[..., CHUNK // 2:CHUNK // 2 + 1, :]
    att = jnp.einsum('bhnik,bhnjk->bhnij', q * jnp.exp(g - g_mid), k * jnp.exp(g_mid - g))
    att = jnp.where(_tri(0), att, 0.0)
    s_loc = jnp.einsum('bhnck,bhncv->bhnkv', k * jnp.exp(g_end[..., None, :] - g), v)

    def step(s, xs):
        s_l, ge = xs
        return jnp.exp(ge)[..., None] * s + s_l, s

    _, s0 = lax.scan(step, jnp.zeros((bsz, nh, dk, dv), jnp.float32), (_to_n(s_loc), _to_n(g_end)))
    s0 = _from_n(s0)
    o = jnp.einsum('bhnik,bhnkv->bhniv', q * jnp.exp(g), s0) + jnp.einsum('bhnij,bhnjv->bhniv', att, v)
    return _unchunk(o)


def _gdn_scan(q, k, v, beta, log_a):
    bsz, _, nh, dk = q.shape
    dv = v.shape[-1]
    q = _chunk(q) * dk ** -0.5
    k, v = _chunk(k), _chunk(v)
    beta = _chunk(beta[..., None])
    g = jnp.cumsum(_chunk(log_a[..., None]), axis=3)
    decay = jnp.exp(jnp.where(_tri(0), g - jnp.swapaxes(g, -1, -2), -jnp.inf))
    m = jnp.where(_tri(-1), beta * jnp.einsum('bhnik,bhnjk->bhnij', k, k) * decay, 0.0)
    rhs = jnp.concatenate([v * beta, k * (beta * jnp.exp(g))], -1)
    sol = lax.linalg.triangular_solve(m, rhs, left_side=True, lower=True, unit_diagonal=True)
    u, w = sol[..., :dv], sol[..., dv:]
    k_end = k * jnp.exp(g[..., -1:, :] - g)

    def step(s, xs):
        u_c, w_c, k_c, ge = xs
        v_new = u_c - jnp.einsum('bhck,bhkv->bhcv', w_c, s)
        return jnp.exp(ge)[..., None, None] * s + jnp.einsum('bhck,bhcv->bhkv', k_c, v_new), s

    xs = (_to_n(u), _to_n(w), _to_n(k_end), _to_n(g[..., -1, 0]))
    _, s0 = lax.scan(step, jnp.zeros((bsz, nh, dk, dv), jnp.float32), xs)
    s0 = _from_n(s0)
    v_new = u - jnp.einsum('bhnck,bhnkv->bhncv', w, s0)
    att = jnp.einsum('bhnik,bhnjk->bhnij', q, k) * decay
    o = jnp.einsum('bhnik,bhnkv->bhniv', q * jnp.exp(g), s0) + jnp.einsum('bhnij,bhnjv->bhniv', att, v_new)
    return _unchunk(o)


def _mlstm_mixer(z, n_ctx, ig_b, fg_b, norm_w):
    bsz, L, _ = z.shape
    q, k, v, o, ig, fg = _split(z, ML_SIZES)
    q = q.reshape(bsz, L, HEADS, HEAD_QK)
    k = k.reshape(bsz, L, HEADS, HEAD_QK)
    v = v.reshape(bsz, L, HEADS, HEAD_V)
    ig = ig.reshape(bsz, L, N_DIR, HEADS) + ig_b
    fg = fg.reshape(bsz, L, N_DIR, HEADS) + fg_b
    h = _bidir(_mlstm_scan, n_ctx, (q, k, v, ig[:, :, 0], fg[:, :, 0]), (q, k, v, ig[:, :, 1], fg[:, :, 1]))
    return _head_rms(h, norm_w) * jax.nn.sigmoid(o)


def _rwkv_mixer(z, n_ctx, mu_prev, mu_next, w0, w_up, a0, a_up, g_up, k_k, k_a, r_k, gn_w, gn_b):
    bsz, L, _ = z.shape
    z_prev, z_next = _token_shift(z, n_ctx)
    z = z + mu_prev * (z_prev - z) + mu_next * (z_next - z)
    r, k, v, wl, al, gl = _split(z, RW_SIZES)

    def heads(t):
        return t.reshape(bsz, L, HEADS, HEAD_V)

    kk = _l2n(heads(k * k_k))
    wl = jnp.tanh(wl)
    g = jax.nn.sigmoid(gl) @ g_up
    dir_args, k_sum = [], 0.0
    for d in range(N_DIR):
        log_w = -RW_DECAY_SCALE * jax.nn.sigmoid(w0[d] + wl @ w_up[d])
        a = jax.nn.sigmoid(a0[d] + al @ a_up[d])
        kt = k * (1 + (a - 1) * k_a)
        k_sum = k_sum + kt
        dir_args.append((heads(r), heads(log_w), heads(kt), heads(v), kk * heads(a), -kk))
    h = _bidir(_rwkv7_scan, n_ctx, dir_args[0], dir_args[1])
    y = _head_groupnorm(h, gn_w, gn_b)
    bonus = jnp.sum(heads(r * k_sum * r_k), -1, keepdims=True) * heads(v)
    return (y + bonus.reshape(bsz, L, GROUP_W)) * g


def _gla_mixer(z, n_ctx, gate_up, gate_b, norm_w):
    bsz, L, _ = z.shape
    q, k, v, al, og = _split(z, GL_SIZES)
    q = q.reshape(bsz, L, HEADS, HEAD_QK)
    k = k.reshape(bsz, L, HEADS, HEAD_QK)
    v = v.reshape(bsz, L, HEADS, HEAD_V)
    log_a = [(jax.nn.log_sigmoid(al @ gate_up[d] + gate_b[d]) / GLA_LOGIT_NORM).reshape(bsz, L, HEADS, HEAD_QK)
             for d in range(N_DIR)]
    h = _bidir(_gla_scan, n_ctx, (q, k, v, log_a[0]), (q, k, v, log_a[1]))
    return _head_rms(h, norm_w) * jax.nn.silu(og)


def _gdn_mixer(z, n_ctx, conv_w, a_log, dt_bias, norm_w):
    bsz, L, _ = z.shape
    qkv, zg, bl, al = _split(z, GD_SIZES)
    qkv = jax.nn.silu(jnp.concatenate([_grid_conv(qkv[:, :n_ctx], n_ctx, conv_w),
                                       _grid_conv(qkv[:, n_ctx:], GRID_W, conv_w)], axis=1))
    q, k, v = (t.reshape(bsz, L, HEADS, HEAD_V) for t in jnp.split(qkv, 3, axis=-1))
    q, k = _l2n(q), _l2n(k)
    beta = jax.nn.sigmoid(bl.reshape(bsz, L, N_DIR, HEADS))
    log_a = -jnp.exp(a_log) * jax.nn.softplus(al.reshape(bsz, L, N_DIR, HEADS) + dt_bias)
    h = _bidir(_gdn_scan, n_ctx, (q, k, v, beta[:, :, 0], log_a[:, :, 0]), (q, k, v, beta[:, :, 1], log_a[:, :, 1]))
    return _head_rms(h, norm_w) * jax.nn.silu(zg)


def _normal(key, shape, scale):
    return jax.random.normal(key, shape, jnp.float32) * scale


def setup_inputs(seed: int = 0) -> dict:
    key = jax.random.key(seed)
    keys = list(jax.random.split(key, 40))
    nk = keys.pop
    D, L2 = D_MODEL, DEPTH
    dt = jnp.exp(jax.random.uniform(nk(), (L2, N_DIR, HEADS), jnp.float32, math.log(1e-3), math.log(0.1)))
    return {
        'x': _normal(nk(), (BATCH, SEQ, D), 1.0),
        'c': _normal(nk(), (BATCH, D), 1.0),
        'ctx': _normal(nk(), (BATCH, CTX_LEN, D), 1.0),
        'c_ctx': _normal(nk(), (D,), 1.0),
        'ada_w': _normal(nk(), (L2, D, N_MOD * D), 0.5 * D ** -0.5),
        'ada_b': _normal(nk(), (L2, N_MOD * D), 0.02),
        'norm1_w': 1.0 + _normal(nk(), (L2, D), 0.02),
        'norm2_w': 1.0 + _normal(nk(), (L2, D), 0.02),
        'w_in': _normal(nk(), (L2, D, P_IN), D ** -0.5),
        'w_out': _normal(nk(), (L2, D, D), D ** -0.5),
        'ml_ig_b': _normal(nk(), (L2, N_DIR, HEADS), 0.1),
        'ml_fg_b': jnp.linspace(3.0, 6.0, HEADS) + _normal(nk(), (L2, N_DIR, HEADS), 0.1),
        'ml_norm_w': 1.0 + _normal(nk(), (L2, GROUP_W), 0.02),
        'rw_mu_prev': jax.random.uniform(nk(), (L2, RW_W), jnp.float32, 0.0, 0.5),
        'rw_mu_next': jax.random.uniform(nk(), (L2, RW_W), jnp.float32, 0.0, 0.5),
        'rw_w0': jax.random.uniform(nk(), (L2, N_DIR, GROUP_W), jnp.float32, -3.0, 3.0),
        'rw_w_up': _normal(nk(), (L2, N_DIR, RW_DECAY_LORA, GROUP_W), RW_DECAY_LORA ** -0.5),
        'rw_a0': _normal(nk(), (L2, N_DIR, GROUP_W), 0.1),
        'rw_a_up': _normal(nk(), (L2, N_DIR, RW_ICLR_LORA, GROUP_W), RW_ICLR_LORA ** -0.5),
        'rw_g_up': _normal(nk(), (L2, RW_GATE_LORA, GROUP_W), RW_GATE_LORA ** -0.5),
        'rw_k_k': 0.85 + _normal(nk(), (L2, GROUP_W), 0.02),
        'rw_k_a': 1.0 + _normal(nk(), (L2, GROUP_W), 0.02),
        'rw_r_k': _normal(nk(), (L2, GROUP_W), 0.1),
        'rw_gn_w': 1.0 + _normal(nk(), (L2, GROUP_W), 0.02),
        'rw_gn_b': _normal(nk(), (L2, GROUP_W), 0.02),
        'gl_gate_up': _normal(nk(), (L2, N_DIR, GLA_GATE_LORA, HEADS * HEAD_QK), GLA_GATE_LORA ** -0.5),
        'gl_gate_b': _normal(nk(), (L2, N_DIR, HEADS * HEAD_QK), 0.1),
        'gl_norm_w': 1.0 + _normal(nk(), (L2, GROUP_W), 0.02),
        'gd_conv_w': _normal(nk(), (L2, GD_CONV, GD_CONV, 3 * GROUP_W), (GD_CONV * GD_CONV) ** -0.5),
        'gd_a_log': jnp.log(jax.random.uniform(nk(), (L2, N_DIR, HEADS), jnp.float32, 1.0, 16.0)),
        'gd_dt_bias': dt + jnp.log(-jnp.expm1(-dt)),
        'gd_norm_w': 1.0 + _normal(nk(), (L2, GROUP_W), 0.02),
        'mlp_w1': _normal(nk(), (L2, D, D_FF), D ** -0.5),
        'mlp_w2': _normal(nk(), (L2, D_FF, D), D_FF ** -0.5),
        'final_norm_w': 1.0 + _normal(nk(), (D,), 0.02),
    }


def reference(x, c, ctx, c_ctx, ada_w, ada_b, norm1_w, norm2_w, w_in, w_out, ml_ig_b, ml_fg_b, ml_norm_w,
              rw_mu_prev, rw_mu_next, rw_w0, rw_w_up, rw_a0, rw_a_up, rw_g_up, rw_k_k, rw_k_a, rw_r_k,
              rw_gn_w, rw_gn_b, gl_gate_up, gl_gate_b, gl_norm_w, gd_conv_w, gd_a_log, gd_dt_bias, gd_norm_w,
              mlp_w1, mlp_w2, final_norm_w):
    n_ctx = ctx.shape[1]
    for layer in range(DEPTH):
        mod_x = jax.nn.silu(c) @ ada_w[layer] + ada_b[layer]
        mod_c = jax.nn.silu(c_ctx) @ ada_w[layer] + ada_b[layer]
        shift1_x, scale1_x, gate1_x, shift2_x, scale2_x, gate2_x = jnp.split(mod_x[:, None, :], N_MOD, axis=-1)
        shift1_c, scale1_c, gate1_c, shift2_c, scale2_c, gate2_c = jnp.split(mod_c, N_MOD, axis=-1)
        hx = _rms(x, norm1_w[layer]) * (1 + scale1_x) + shift1_x
        hc = _rms(ctx, norm1_w[layer]) * (1 + scale1_c) + shift1_c
        z = (jnp.concatenate([hc, hx], axis=1) @ w_in[layer]).astype(jnp.float32)
        z_ml, z_rw, z_gl, z_gd = _split(z, (ML_W, RW_W, GL_W, GD_W))
        mixed = jnp.concatenate([
            _mlstm_mixer(z_ml, n_ctx, ml_ig_b[layer], ml_fg_b[layer], ml_norm_w[layer]),
            _rwkv_mixer(z_rw, n_ctx, rw_mu_prev[layer], rw_mu_next[layer], rw_w0[layer], rw_w_up[layer],
                        rw_a0[layer], rw_a_up[layer], rw_g_up[layer], rw_k_k[layer], rw_k_a[layer],
                        rw_r_k[layer], rw_gn_w[layer], rw_gn_b[layer]),
            _gla_mixer(z_gl, n_ctx, gl_gate_up[layer], gl_gate_b[layer], gl_norm_w[layer]),
            _gdn_mixer(z_gd, n_ctx, gd_conv_w[layer], gd_a_log[layer], gd_dt_bias[layer], gd_norm_w[layer]),
        ], axis=-1).astype(x.dtype)
        x = x + gate1_x * (mixed[:, n_ctx:] @ w_out[layer])
        x = x + gate2_x * _mlp(_rms(x, norm2_w[layer]) * (1 + scale2_x) + shift2_x, mlp_w1[layer], mlp_w2[layer])
        if layer < DEPTH - 1:
            ctx = ctx + gate1_c * (mixed[:, :n_ctx] @ w_out[layer])
            ctx = ctx + gate2_c * _mlp(_rms(ctx, norm2_w[layer]) * (1 + scale2_c) + shift2_c,
                                       mlp_w1[layer], mlp_w2[layer])
    return _rms(x, final_norm_w)
```

```python
import math
from contextlib import ExitStack
import numpy as np
import concourse.bass as bass
import concourse.mybir as mybir
from concourse.bass_utils import run_bass_kernel_spmd

F32 = mybir.dt.float32
BF16 = mybir.dt.bfloat16
AF = mybir.ActivationFunctionType
ALU = mybir.AluOpType
AX = mybir.AxisListType

D = 1024
DEPTH = 2
P_IN = 3632
D_FF = 4096
EPS = 1e-6
GN_EPS = 64e-5
RW_C = math.exp(-0.5)
ENG_NAMES = ("pe", "dve", "act", "pool", "sp")


class Res:
    __slots__ = ("name", "w", "r", "dsem", "dcnt")

    def __init__(self, name):
        self.name = name
        self.w = {}
        self.r = {}
        self.dsem = None
        self.dcnt = 0


class Buf:
    __slots__ = ("t", "r")

    def __init__(self, t, r):
        self.t = t
        self.r = r

    def __getitem__(self, k):
        return self.t[k]


class Sched:
    def __init__(self, nc, stack):
        self.nc = nc
        self.engs = {"pe": nc.tensor, "dve": nc.vector, "act": nc.scalar,
                     "pool": nc.gpsimd, "sp": nc.sync}
        self.semh = {}
        for e in ENG_NAMES:
            self.semh["E" + e] = stack.enter_context(nc.semaphore("cs_" + e))
        self.stack = stack
        self.cnt = {e: 0 for e in ENG_NAMES}
        self.prog = {e: [] for e in ENG_NAMES}
        self.seen = {e: {} for e in ENG_NAMES}
        self.free_dsems = []
        self.dsem_cnt = {}
        self.ndsem = 0
        self.phase_dma_res = []
        self.ninstr = 0

    def res(self, name="r"):
        return Res(name)

    def _dsem(self, R):
        if R.dsem is None:
            if self.free_dsems:
                key = self.free_dsems.pop()
            else:
                self.ndsem += 1
                key = "D%d" % self.ndsem
                self.semh[key] = self.stack.enter_context(self.nc.semaphore("ds_%d" % self.ndsem))
                self.dsem_cnt[key] = 0
            R.dsem = key
            R.dcnt = self.dsem_cnt[key]
            self.phase_dma_res.append(R)
        return R.dsem

    @staticmethod
    def _merge(d, s):
        for k, v in s.items():
            if d.get(k, 0) < v:
                d[k] = v

    def _deps(self, eng, reads, writes):
        deps = {}
        other = {}
        for R in reads:
            self._merge(deps, R.w)
        for R in writes:
            self._merge(other, R.w)
            self._merge(other, R.r)
        own = "E" + eng
        for k, v in other.items():
            if k == own:
                continue
            if deps.get(k, 0) < v:
                deps[k] = v
        if eng == "pe" and own in deps:
            del deps[own]
        seen = self.seen[eng]
        out = []
        for k, v in deps.items():
            if seen.get(k, 0) < v:
                seen[k] = v
                out.append((k, v))
        return out

    def _post(self, tok, reads, writes, partial):
        for R in reads:
            if R.r.get(tok[0], 0) < tok[1]:
                R.r[tok[0]] = tok[1]
        for R in writes:
            if partial:
                R.w[tok[0]] = tok[1]
            else:
                R.w = {tok[0]: tok[1]}
            R.r = {}

    def op(self, eng, fn, reads=(), writes=(), partial=False):
        waits = self._deps(eng, reads, writes)
        self.cnt[eng] += 1
        tok = ("E" + eng, self.cnt[eng])
        self.prog[eng].append((waits, fn, tok))
        self._post(tok, reads, writes, partial)
        self.ninstr += 1

    def dma(self, q, fn, sb, reads=(), writes=(), partial=False):
        waits = self._deps(q, reads, writes)
        key = self._dsem(sb)
        sb.dcnt += 16
        self.dsem_cnt[key] = sb.dcnt
        tok = (key, sb.dcnt)
        self.prog[q].append((waits, fn, tok))
        self._post(tok, reads, writes, partial)
        self.ninstr += 1

    def end_phase(self):
        allw = [("E" + e, self.cnt[e]) for e in ENG_NAMES if self.cnt[e] > 0]
        for R in self.phase_dma_res:
            allw.append((R.dsem, self.dsem_cnt[R.dsem]))
        allw = list(dict(allw).items())
        for e in ENG_NAMES:
            seen = self.seen[e]
            ws = []
            for k, v in allw:
                if k == "E" + e:
                    continue
                if seen.get(k, 0) < v:
                    seen[k] = v
                    ws.append((k, v))
            self.prog[e].append((ws, None, None))
        self.emit()
        for R in self.phase_dma_res:
            if R.dsem not in self.free_dsems:
                self.free_dsems.append(R.dsem)
            R.dsem = None
        self.phase_dma_res = []

    def emit(self):
        nc = self.nc
        semh = self.semh
        prog = self.prog

        def run(engname, e):
            for waits, fn, tok in prog[engname]:
                for k, v in waits:
                    e.wait_ge(semh[k], v)
                if fn is None:
                    continue
                ins = fn(e)
                ins.then_inc(semh[tok[0]], 1 if tok[0][0] == "E" else 16)

        with nc.Block() as block:
            @block.tensor
            def _(e):
                run("pe", e)

            @block.vector
            def _(e):
                run("dve", e)

            @block.scalar
            def _(e):
                run("act", e)

            @block.gpsimd
            def _(e):
                run("pool", e)

            @block.sync
            def _(e):
                run("sp", e)
        self.prog = {e: [] for e in ENG_NAMES}


class K:
    def __init__(self, nc, S, gst):
        self.nc = nc
        self.S = S
        self.gst = gst
        self.st = None
        self.rr = 0

    def sb(self, name, shape, dt=F32, glob=False):
        st = self.gst if glob else self.st
        self.rr += 1
        name = "%s_%d" % (name, self.rr)
        return Buf(st.enter_context(self.nc.sbuf_tensor(name, list(shape), dt)), self.S.res(name))

    def ps(self, name, shape, dt=F32):
        self.rr += 1
        name = "%s_%d" % (name, self.rr)
        return Buf(self.st.enter_context(self.nc.psum_tensor(name, list(shape), dt)), self.S.res(name))

    @staticmethod
    def _r(bs):
        return [b.r for b in bs]

    def mm(self, out, lhsT, rhs, reads, writes, start=True, stop=True):
        self.S.op("pe", lambda e: e.matmul(out, lhsT, rhs, start=start, stop=stop),
                  self._r(reads), self._r(writes), partial=True)

    def tr(self, out, in_, ident, reads, writes):
        self.S.op("pe", lambda e: e.transpose(out, in_, ident), self._r(reads), self._r(writes), partial=True)

    def act(self, out, in_, func, reads, writes, bias=0.0, scale=1.0, partial=False):
        self.S.op("act", lambda e: e.activation(out=out, in_=in_, func=func, bias=bias, scale=scale),
                  self._r(reads), self._r(writes), partial=partial)

    def tt(self, eng, out, in0, in1, op, reads, writes, partial=False):
        self.S.op(eng, lambda e: e.tensor_tensor(out=out, in0=in0, in1=in1, op=op),
                  self._r(reads), self._r(writes), partial=partial)

    def ts(self, eng, out, in0, s1, op0, reads, writes, s2=None, op1=None, partial=False):
        if op1 is None:
            self.S.op(eng, lambda e: e.tensor_scalar(out=out, in0=in0, scalar1=s1, scalar2=None, op0=op0),
                      self._r(reads), self._r(writes), partial=partial)
        else:
            self.S.op(eng, lambda e: e.tensor_scalar(out=out, in0=in0, scalar1=s1, scalar2=s2, op0=op0, op1=op1),
                      self._r(reads), self._r(writes), partial=partial)

    def stt(self, eng, out, in0, scalar, in1, op0, op1, reads, writes, partial=False):
        self.S.op(eng, lambda e: e.scalar_tensor_tensor(out=out, in0=in0, scalar=scalar, in1=in1, op0=op0, op1=op1),
                  self._r(reads), self._r(writes), partial=partial)

    def cp(self, eng, out, in_, reads, writes, partial=False):
        if eng == "act":
            self.S.op("act", lambda e: e.copy(out=out, in_=in_), self._r(reads), self._r(writes), partial=partial)
        else:
            self.S.op(eng, lambda e: e.tensor_copy(out=out, in_=in_), self._r(reads), self._r(writes),
                      partial=partial)

    def red(self, eng, out, in_, reads, writes, op=ALU.add, partial=False):
        self.S.op(eng, lambda e: e.tensor_reduce(out=out, in_=in_, axis=AX.X, op=op),
                  self._r(reads), self._r(writes), partial=partial)

    def recip(self, out, in_, reads, writes, partial=False):
        self.S.op("dve", lambda e: e.reciprocal(out=out, in_=in_), self._r(reads), self._r(writes), partial=partial)

    def memset(self, eng, out, val, writes, partial=False):
        self.S.op(eng, lambda e: e.memset(out, val), [], self._r(writes), partial=partial)

    def aselect(self, out, in_, pattern, cmp, fill, base, cm, reads, writes):
        self.S.op("pool", lambda e: e.affine_select(out=out, in_=in_, pattern=pattern, compare_op=cmp, fill=fill,
                                                    base=base, channel_multiplier=cm),
                  self._r(reads), self._r(writes))

    def load(self, out, in_, sbuf, src=None, q="sp", partial=False, slow=False):
        reads = [src.r] if src is not None else []
        if slow:
            fn = lambda e: e.dma_start(out=out, in_=in_, allow_slow_non_contiguous=True)
        else:
            fn = lambda e: e.dma_start(out=out, in_=in_)
        self.S.dma(q, fn, sbuf.r, reads=reads, writes=[sbuf.r], partial=partial)

    def store(self, out, in_, sbuf, dst, q="pool"):
        self.S.dma(q, lambda e: e.dma_start(out=out, in_=in_), sbuf.r, reads=[sbuf.r], writes=[dst.r],
                   partial=True)

    def rsqrt(self, out, in_, reads, writes, scale=1.0, eps=EPS):
        self.act(out, in_, AF.Sqrt, reads, writes, bias=eps, scale=scale)
        self.recip(out, out, writes, writes)


def rot(lst, i):
    return lst[i % len(lst)]


def build(cfg):
    NCT = cfg.get("nctx", 2)
    NLT = cfg.get("nlat", 32)
    NT = NCT + NLT
    L = NT * 128
    NCTX = NCT * 128
    phases = cfg.get("phases", None)
    layers = cfg.get("layers", list(range(DEPTH)))
    io = cfg.get("io", {})
    CUT = cfg.get("cut", 99)

    nc = bass.Bass("TRN2", target_bir_lowering=False)

    def dram(name, shape, kind):
        kind = io.get(name, kind)
        return Buf(nc.dram_tensor(name, list(shape), F32, kind=kind).ap(), Res(name))

    EI, EO, IN = "ExternalInput", "ExternalOutput", "Internal"
    d_x = dram("x", [NLT * 128, D], EI)
    d_c = dram("c", [1, D], EI)
    d_ctx = dram("ctx", [NCTX, D], EI)
    d_cctx = dram("c_ctx", [1, D], EI)
    prm = {}
    for name, shape in [("ada_w", [DEPTH, D, 6 * D]), ("ada_b", [DEPTH, 6 * D]), ("norm1_w", [DEPTH, D]),
                        ("norm2_w", [DEPTH, D]), ("w_in", [DEPTH, D, P_IN]), ("w_out", [DEPTH, D, D]),
                        ("ml_ig_b", [DEPTH, 8]), ("ml_fg_b", [DEPTH, 8]), ("ml_norm_w", [DEPTH, 256]),
                        ("rw_mu_prev", [DEPTH, 1024]), ("rw_mu_next", [DEPTH, 1024]), ("rw_w0", [DEPTH, 2, 256]),
                        ("rw_w_up", [DEPTH, 2, 64, 256]), ("rw_a0", [DEPTH, 2, 256]),
                        ("rw_a_up", [DEPTH, 2, 64, 256]), ("rw_g_up", [DEPTH, 128, 256]),
                        ("rw_k_k", [DEPTH, 256]), ("rw_k_a", [DEPTH, 256]), ("rw_r_k", [DEPTH, 256]),
                        ("rw_gn_w", [DEPTH, 256]), ("rw_gn_b", [DEPTH, 256]),
                        ("gl_gate_up", [DEPTH, 2, 16, 128]), ("gl_gate_b", [DEPTH, 2, 128]),
                        ("gl_norm_w", [DEPTH, 256]), ("gd_conv_w", [DEPTH, 9, 768]), ("gd_a_log", [DEPTH, 8]),
                        ("gd_dt_bias", [DEPTH, 8]), ("gd_norm_w", [DEPTH, 256]),
                        ("mlp_w1", [DEPTH, D, D_FF]), ("mlp_w2", [DEPTH, D_FF, D]), ("final_norm_w", [1, D])]:
        prm[name] = dram(name, shape, EI)
    d_out = dram("out", [NLT * 128, D], EO)
    d_xs = dram("xs", [L, D], IN)
    d_z = dram("z", [L, P_IN], IN)
    d_mixed = dram("mixed", [L, D], IN)
    d_qkvc = dram("qkvc", [L, 768], IN)
    RWP_W = 5 * 256 + 6 * 256
    d_rwp = dram("rwp", [L, RWP_W], IN)

    def on(ph):
        return phases is None or ph in phases

    with ExitStack() as gst:
        S = Sched(nc, gst)
        k = K(nc, S, gst)
        ident = k.sb("ident", [128, 128], glob=True)
        identb = k.sb("identb", [128, 128], BF16, glob=True)
        ones = k.sb("ones", [128, 128], glob=True)
        mU = k.sb("mU", [128, 128], glob=True)
        mL = k.sb("mL", [128, 128], glob=True)
        mUs = k.sb("mUs", [128, 128], glob=True)
        mLs = k.sb("mLs", [128, 128], glob=True)
        modcol = k.sb("modcol", [128, 48, 2], glob=True)
        gatebc = k.sb("gatebc", [128, 4, 1024], glob=True)

        k.st = gst
        k.memset("pool", ident[:], 0.0, [ident])
        k.aselect(ident[:], ident[:], [[-1, 128]], ALU.not_equal, 1.0, 0, 1, [ident], [ident])
        k.cp("pool", identb[:], ident[:], [ident], [identb])
        k.memset("pool", ones[:], 1.0, [ones])
        for m, (pat, cm, cmp) in ((mU, ([[1, 128]], -1, ALU.is_ge)), (mL, ([[-1, 128]], 1, ALU.is_ge)),
                                  (mUs, ([[1, 128]], -1, ALU.is_gt)), (mLs, ([[-1, 128]], 1, ALU.is_gt))):
            k.memset("pool", m[:], 1.0, [m])
            k.aselect(m[:], m[:], pat, cmp, 0.0, 0, cm, [m], [m])
        CUM = [mU, mL]
        CUMS = [mUs, mLs]
        OPP_S = [mLs, mUs]
        S.end_phase()

        def tile_order(d):
            if d == 0:
                return list(range(NT))
            return list(range(NCT - 1, -1, -1)) + list(range(NT - 1, NCT - 1, -1))

        def seg_j(n):
            return 1 if n < NCT else 0

        def phase_init_xs():
            with ExitStack() as st:
                k.st = st
                bufs = [k.sb("ix%d" % i, [128, D]) for i in range(3)]
                for n in range(NT):
                    b = rot(bufs, n)
                    if n < NCT:
                        k.load(b[:], d_ctx[n * 128:(n + 1) * 128, :], b, src=d_ctx)
                    else:
                        k.load(b[:], d_x[(n - NCT) * 128:(n - NCT + 1) * 128, :], b, src=d_x)
                    k.store(d_xs[n * 128:(n + 1) * 128, :], b[:], b, d_xs)
                S.end_phase()

        def phase_mod(l):
            with ExitStack() as st:
                k.st = st
                craw = k.sb("craw", [128, 8, 2])
                sc = k.sb("sc", [128, 8, 2])
                scb = k.sb("scb", [128, 16, 128])
                adab_col = k.sb("adab_col", [128, 48])
                n1col = k.sb("n1col", [128, 8])
                n2col = k.sb("n2col", [128, 8])
                abg = k.sb("abg", [128, 2, 1024])
                aw = [k.sb("aw%d" % i, [128, 8, 512]) for i in range(2)]
                pm = k.ps("pm", [128, 48, 2])
                pg = [k.ps("pg%d" % i, [128, 512]) for i in range(2)]
                k.load(craw[:, :, 0], d_c[0, :].rearrange("(kc p) -> p kc", p=128), craw, src=d_c, slow=True,
                       partial=True)
                k.load(craw[:, :, 1], d_cctx[0, :].rearrange("(kc p) -> p kc", p=128), craw, src=d_cctx, slow=True,
                       partial=True)
                k.load(adab_col[:], prm["ada_b"][l, :].rearrange("(nt p) -> p nt", p=128), adab_col,
                       src=prm["ada_b"], slow=True)
                k.load(n1col[:], prm["norm1_w"][l, :].rearrange("(kc p) -> p kc", p=128), n1col,
                       src=prm["norm1_w"], slow=True)
                k.load(n2col[:], prm["norm2_w"][l, :].rearrange("(kc p) -> p kc", p=128), n2col,
                       src=prm["norm2_w"], slow=True)
                k.load(abg[:, 0, :], prm["ada_b"][l, 2048:3072].partition_broadcast(128), abg, src=prm["ada_b"],
                       partial=True)
                k.load(abg[:, 1, :], prm["ada_b"][l, 5120:6144].partition_broadcast(128), abg, src=prm["ada_b"],
                       partial=True)
                k.act(sc[:], craw[:], AF.Silu, [craw], [sc])
                k.cp("dve", scb[:], sc[:].rearrange("p a b -> p (a b)").unsqueeze(2).to_broadcast([128, 16, 128]),
                     [sc], [scb])
                for nchunk in range(12):
                    a = rot(aw, nchunk)
                    k.load(a[:], prm["ada_w"][l, :, nchunk * 512:(nchunk + 1) * 512].rearrange(
                        "(kc p) n -> p kc n", p=128), a, src=prm["ada_w"])
                    for sub in range(4):
                        nt = nchunk * 4 + sub
                        for kc in range(8):
                            k.mm(pm[:, nt, :], a[:, kc, sub * 128:(sub + 1) * 128], sc[:, kc, :], [a, sc], [pm],
                                 start=(kc == 0), stop=(kc == 7))
                    if nchunk in (4, 5, 10, 11):
                        g = 0 if nchunk < 6 else 1
                        half = nchunk % 2
                        for j in range(2):
                            p = rot(pg, j)
                            for kc in range(8):
                                k.mm(p[:], scb[:, kc * 2 + j, :], a[:, kc, :], [scb, a], [p],
                                     start=(kc == 0), stop=(kc == 7))
                            k.tt("dve", gatebc[:, g * 2 + j, half * 512:(half + 1) * 512], p[:],
                                 abg[:, g, half * 512:(half + 1) * 512], ALU.add, [p, abg], [gatebc], partial=True)
                k.tt("dve", modcol[:], pm[:], adab_col[:].unsqueeze(2).to_broadcast([128, 48, 2]), ALU.add,
                     [pm, adab_col], [modcol])
                for m, ncol in ((1, n1col), (4, n2col)):
                    k.stt("dve", modcol[:, m * 8:(m + 1) * 8, :], modcol[:, m * 8:(m + 1) * 8, :], 1.0,
                          ncol[:].unsqueeze(2).to_broadcast([128, 8, 2]), ALU.add, ALU.mult,
                          [modcol, ncol], [modcol])
                S.end_phase()

        def norm_to_T(xt, xtb, m_scale, m_shift, j, hT_dst, scr):
            sq, ssq, xn, pT = scr["sq"], scr["ssq"], scr["xn"], scr["pT"]
            k.tt("pool", sq[:], xt, xt, ALU.mult, [xtb], [sq])
            k.red("dve", ssq[:], sq[:], [sq], [ssq])
            k.rsqrt(ssq[:], ssq[:], [ssq], [ssq], scale=1.0 / D, eps=EPS)
            k.ts("dve", xn[:], xt, ssq[:, 0:1], ALU.mult, [xtb, ssq], [xn])
            for kc in range(8):
                k.tr(pT[:, kc * 128:(kc + 1) * 128], xn[:, kc * 128:(kc + 1) * 128], identb[:], [xn, identb], [pT])
            for kc in range(8):
                k.ts("dve", hT_dst(kc), pT[:, kc * 128:(kc + 1) * 128], modcol[:, m_scale * 8 + kc, j:j + 1], ALU.mult,
                     [pT, modcol], [scr["dst"]], s2=modcol[:, m_shift * 8 + kc, j:j + 1], op1=ALU.add, partial=True)

        def phase_A(l):
            with ExitStack() as st:
                k.st = st
                wb = k.sb("winb", [128, 8, P_IN], BF16)
                stg = [k.sb("wstg%d" % i, [128, P_IN]) for i in range(2)]
                for kc in range(8):
                    s = rot(stg, kc)
                    k.load(s[:], prm["w_in"][l, kc * 128:(kc + 1) * 128, :], s, src=prm["w_in"])
                    k.cp("pool" if kc % 2 else "dve", wb[:, kc, :], s[:], [s], [wb], partial=True)
                xts = [k.sb("xa%d" % i, [128, D]) for i in range(2)]
                scrs = []
                for i in range(2):
                    scrs.append({"sq": k.sb("sqa%d" % i, [128, D]), "ssq": k.sb("ssqa%d" % i, [128, 1]),
                                 "xn": k.sb("xna%d" % i, [128, D], BF16),
                                 "pT": k.ps("pTa%d" % i, [128, D], BF16),
                                 "dst": k.sb("hTa%d" % i, [128, 8, 128], BF16)})
                zts = [k.sb("zt%d" % i, [128, P_IN]) for i in range(2)]
                pz = [k.ps("pz%d" % i, [128, 512]) for i in range(4)]
                ci = 0
                for n in range(NT):
                    xt = rot(xts, n)
                    scr = rot(scrs, n)
                    zt = rot(zts, n)
                    hT = scr["dst"]
                    k.load(xt[:], d_xs[n * 128:(n + 1) * 128, :], xt, src=d_xs)
                    norm_to_T(xt[:], xt, 1, 0, seg_j(n), lambda kc: hT[:, kc, :], scr)
                    n0 = 0
                    while n0 < P_IN:
                        nsz = min(512, P_IN - n0)
                        p = rot(pz, ci)
                        for kc in range(8):
                            k.mm(p[:, 0:nsz], hT[:, kc, :], wb[:, kc, n0:n0 + nsz], [hT, wb], [p],
                                 start=(kc == 0), stop=(kc == 7))
                        k.cp("act" if ci % 2 else "dve", zt[:, n0:n0 + nsz], p[:, 0:nsz], [p], [zt], partial=True)
                        ci += 1
                        n0 += nsz
                    k.store(d_z[n * 128:(n + 1) * 128, :], zt[:], zt, d_z)
                S.end_phase()

        def phase_B0(l, last):
            with ExitStack() as st:
                k.st = st
                wb = k.sb("woutb", [128, 8, D], BF16)
                stg = [k.sb("wostg%d" % i, [128, D]) for i in range(2)]
                for kc in range(8):
                    s = rot(stg, kc)
                    k.load(s[:], prm["w_out"][l, kc * 128:(kc + 1) * 128, :], s, src=prm["w_out"])
                    k.cp("pool" if kc % 2 else "dve", wb[:, kc, :], s[:], [s], [wb], partial=True)
                mts = [k.sb("mt%d" % i, [128, D]) for i in range(2)]
                mbs = [k.sb("mb%d" % i, [128, D], BF16) for i in range(2)]
                mTs = [k.sb("mT%d" % i, [128, D], BF16) for i in range(2)]
                pTs = [k.ps("pTb%d" % i, [128, D], BF16) for i in range(2)]
                xts = [k.sb("xb%d" % i, [128, D]) for i in range(2)]
                tmps = [k.sb("tb%d" % i, [128, D]) for i in range(2)]
                po = [k.ps("po%d" % i, [128, 512]) for i in range(4)]
                ci = 0
                for n in range(NT):
                    if last and n < NCT:
                        continue
                    mt, mb, mT, pT, xt, tmp = (rot(a, n) for a in (mts, mbs, mTs, pTs, xts, tmps))
                    k.load(mt[:], d_mixed[n * 128:(n + 1) * 128, :], mt, src=d_mixed)
                    k.load(xt[:], d_xs[n * 128:(n + 1) * 128, :], xt, src=d_xs)
                    k.cp("pool", mb[:], mt[:], [mt], [mb])
                    for kc in range(8):
                        k.tr(pT[:, kc * 128:(kc + 1) * 128], mb[:, kc * 128:(kc + 1) * 128], identb[:], [mb, identb],
                             [pT])
                    k.cp("act", mT[:], pT[:], [pT], [mT])
                    for nh in range(2):
                        p = rot(po, ci)
                        ci += 1
                        for kc in range(8):
                            k.mm(p[:], mT[:, kc * 128:(kc + 1) * 128], wb[:, kc, nh * 512:(nh + 1) * 512], [mT, wb],
                                 [p], start=(kc == 0), stop=(kc == 7))
                        k.tt("dve", tmp[:, nh * 512:(nh + 1) * 512], p[:],
                             gatebc[:, 0 + seg_j(n), nh * 512:(nh + 1) * 512], ALU.mult, [p, gatebc], [tmp],
                             partial=True)
                    k.tt("pool", tmp[:], tmp[:], xt[:], ALU.add, [tmp, xt], [tmp])
                    k.store(d_xs[n * 128:(n + 1) * 128, :], tmp[:], tmp, d_xs)
                S.end_phase()

        def phase_B1(l, last):
            with ExitStack() as st:
                k.st = st
                w1b = k.sb("w1b", [128, 8, D_FF], BF16)
                w2b = k.sb("w2b", [128, 32, D], BF16)
                stg = [k.sb("w12stg%d" % i, [128, 1024]) for i in range(2)]
                si = 0
                for kc in range(8):
                    for q in range(4):
                        s = rot(stg, si)
                        k.load(s[:], prm["mlp_w1"][l, kc * 128:(kc + 1) * 128, q * 1024:(q + 1) * 1024], s,
                               src=prm["mlp_w1"])
                        k.cp("pool" if si % 2 else "dve", w1b[:, kc, q * 1024:(q + 1) * 1024], s[:], [s], [w1b],
                             partial=True)
                        si += 1
                for f in range(32):
                    s = rot(stg, si)
                    k.load(s[:], prm["mlp_w2"][l, f * 128:(f + 1) * 128, :], s, src=prm["mlp_w2"])
                    k.cp("pool" if si % 2 else "dve", w2b[:, f, :], s[:], [s], [w2b], partial=True)
                    si += 1
                fnw = k.sb("fnw", [128, D])
                if last:
                    k.load(fnw[:], prm["final_norm_w"][0, :].partition_broadcast(128), fnw, src=prm["final_norm_w"])
                GT = 2
                x1g = [k.sb("x1g%d" % i, [128, GT, D]) for i in range(1)]
                h2Ts = [k.sb("h2T%d" % i, [128, 8, GT * 128], BF16) for i in range(1)]
                uTs = [k.sb("uT%d" % i, [128, 32, GT * 128], BF16) for i in range(1)]
                x2s = [k.sb("x2_%d" % i, [128, D]) for i in range(1)]
                tmp2 = [k.sb("t2_%d" % i, [128, D]) for i in range(1)]
                scr = {"sq": tmp2[0], "ssq": k.sb("ssqb", [128, 1]),
                       "xn": k.sb("xnb", [128, D], BF16), "pT": k.ps("pTc", [128, D], BF16)}
                rl = [k.sb("rl%d" % i, [128, GT * 128]) for i in range(2)]
                pu = [k.ps("pu%d" % i, [128, 512]) for i in range(3)]
                py = [k.ps("py%d" % i, [128, 512]) for i in range(2)]
                groups = []
                if not last:
                    for g0 in range(0, NCT, GT):
                        groups.append(list(range(g0, min(NCT, g0 + GT))))
                for g0 in range(NCT, NT, GT):
                    groups.append(list(range(g0, min(NT, g0 + GT))))
                ci = 0
                for gi, grp in enumerate(groups):
                    xg, h2T, uT = rot(x1g, gi), rot(h2Ts, gi), rot(uTs, gi)
                    j = seg_j(grp[0])
                    ntok = len(grp) * 128
                    scr["dst"] = h2T
                    for ti, n in enumerate(grp):
                        k.load(xg[:, ti, :], d_xs[n * 128:(n + 1) * 128, :], xg, src=d_xs, partial=True)
                        norm_to_T(xg[:, ti, :], xg, 4, 3, j, lambda kc, ti=ti: h2T[:, kc, ti * 128:(ti + 1) * 128], scr)
                    for f in range(32):
                        p = rot(pu, f)
                        r = rot(rl, f)
                        for kc in range(8):
                            k.mm(p[:, 0:ntok], w1b[:, kc, f * 128:(f + 1) * 128], h2T[:, kc, 0:ntok], [w1b, h2T], [p],
                                 start=(kc == 0), stop=(kc == 7))
                        k.act(r[:, 0:ntok], p[:, 0:ntok], AF.Relu, [p], [r])
                        k.tt("pool" if f % 2 else "dve", uT[:, f, 0:ntok], r[:, 0:ntok], r[:, 0:ntok], ALU.mult, [r],
                             [uT], partial=True)
                    for ti, n in enumerate(grp):
                        x2, t2 = rot(x2s, ci), rot(tmp2, ci)
                        ci += 1
                        for nh in range(2):
                            p = rot(py, nh)
                            for f in range(32):
                                k.mm(p[:], uT[:, f, ti * 128:(ti + 1) * 128], w2b[:, f, nh * 512:(nh + 1) * 512],
                                     [uT, w2b], [p], start=(f == 0), stop=(f == 31))
                            k.tt("dve", t2[:, nh * 512:(nh + 1) * 512], p[:], gatebc[:, 2 + j, nh * 512:(nh + 1) * 512],
                                 ALU.mult, [p, gatebc], [t2], partial=True)
                        k.tt("pool", x2[:], t2[:], xg[:, ti, :], ALU.add, [t2, xg], [x2])
                        if last:
                            k.tt("pool", t2[:], x2[:], x2[:], ALU.mult, [x2], [t2])
                            k.red("dve", scr["ssq"][:], t2[:], [t2], [scr["ssq"]])
                            k.rsqrt(scr["ssq"][:], scr["ssq"][:], [scr["ssq"]], [scr["ssq"]], scale=1.0 / D)
                            k.stt("dve", t2[:], x2[:], scr["ssq"][:, 0:1], fnw[:], ALU.mult, ALU.mult,
                                  [x2, scr["ssq"], fnw], [t2])
                            k.store(d_out[(n - NCT) * 128:(n - NCT + 1) * 128, :], t2[:], t2, d_out)
                        else:
                            k.store(d_xs[n * 128:(n + 1) * 128, :], x2[:], x2, d_xs)
                S.end_phase()

        def head_rms_post(osum, n, gain_bc, gate_in, gate_func, col0, tmpb, ssb, outb, gateb, gate_reads):
            k.tt("pool", tmpb[:], osum[:], osum[:], ALU.mult, [osum], [tmpb])
            k.red("dve", ssb[:], tmpb[:].rearrange("p (h d) -> p h d", h=4), [tmpb], [ssb])
            k.rsqrt(ssb[:], ssb[:], [ssb], [ssb], scale=1.0 / 64, eps=EPS)
            k.tt("dve", outb[:].rearrange("p (h d) -> p h d", h=4), osum[:].rearrange("p (h d) -> p h d", h=4),
                 ssb[:].unsqueeze(2).to_broadcast([128, 4, 64]), ALU.mult, [osum, ssb], [outb])
            k.tt("pool", outb[:], outb[:], gain_bc[:], ALU.mult, [outb, gain_bc], [outb])
            k.act(gateb[:], gate_in, gate_func, gate_reads, [gateb])
            k.tt("dve", outb[:], outb[:], gateb[:], ALU.mult, [outb, gateb], [outb])
            k.store(d_mixed[n * 128:(n + 1) * 128, col0:col0 + 256], outb[:], outb, d_mixed)

        def softplus(out, in_, reads, writes, scale=1.0):
            k.act(out, in_, AF.Exp, reads, writes, scale=scale)
            k.act(out, out, AF.Ln, writes, writes, bias=1.0)

        def solve(Xs, Ns, NTs, N2s, NT2s, pss):
            nh = len(Xs)
            cur = [(Ns[i], NTs[i]) for i in range(nh)]
            nxt = [(N2s[i], NT2s[i]) for i in range(nh)]
            for step in range(7):
                for i in range(nh):
                    N_, NT_ = cur[i]
                    pa, pb, pc = pss[i]
                    k.mm(pa[:, 0:128], NT_[:], Xs[i][:], [NT_, Xs[i]], [pa])
                    if step < 6:
                        k.mm(pb[:, 0:128], NT_[:], N_[:], [NT_, N_], [pb])
                        k.mm(pc[:, 0:128], N_[:], NT_[:], [N_, NT_], [pc])
                for i in range(nh):
                    pa, pb, pc = pss[i]
                    k.tt("dve", Xs[i][:], Xs[i][:], pa[:, 0:128], ALU.add, [Xs[i], pa], [Xs[i]])
                    if step < 6:
                        k.cp("act", nxt[i][0][:], pb[:, 0:128], [pb], [nxt[i][0]])
                        k.cp("dve" if i % 2 else "act", nxt[i][1][:], pc[:, 0:128], [pc], [nxt[i][1]])
                cur, nxt = nxt, cur

        def phase_gla(l):
            Z0 = 784 + 1024
            ZW = 784
            sc = 32 ** -0.5
            with ExitStack() as st:
                k.st = st
                hsum = k.sb("hsum", [128, NT, 256])
                gup = k.sb("gup", [16, 2, 128])
                gbb = k.sb("gbb", [128, 2, 128])
                nwb = k.sb("nwb", [128, 256])
                for d in range(2):
                    k.load(gup[:, d, :], prm["gl_gate_up"][l, d, :, :], gup, src=prm["gl_gate_up"], partial=True)
                    k.load(gbb[:, d, :], prm["gl_gate_b"][l, d, :].partition_broadcast(128), gbb,
                           src=prm["gl_gate_b"], partial=True)
                k.load(nwb[:], prm["gl_norm_w"][l, :].partition_broadcast(128), nwb, src=prm["gl_norm_w"])
                zts = [k.sb("zg%d" % i, [128, ZW]) for i in range(2)]
                Sst = [k.sb("Sg%d" % h, [32, 64]) for h in range(4)]
                nb = 2
                alT = [k.sb("alT%d" % i, [16, 128]) for i in range(nb)]
                spb = [k.sb("spb%d" % i, [128, 128]) for i in range(nb)]
                ex = [k.sb("ex%d" % i, [128, 3, 128]) for i in range(nb)]
                qk = [k.sb("qk%d" % i, [128, 3, 128]) for i in range(nb)]
                egend = [k.sb("egend%d" % i, [32, 4]) for i in range(nb)]
                qkT = [k.sb("qkT%d" % i, [32, 2, 128]) for i in range(4)]
                attT = [k.sb("attT%d" % i, [128, 128]) for i in range(4)]
                osum = [k.sb("osum%d" % i, [128, 256]) for i in range(nb)]
                tmpb = [k.sb("tmpb%d" % i, [128, 256]) for i in range(nb)]
                ssb = [k.sb("ssb%d" % i, [128, 4]) for i in range(nb)]
                outb = [k.sb("outb%d" % i, [128, 256]) for i in range(nb)]
                gateb = [k.sb("gateb%d" % i, [128, 256]) for i in range(nb)]
                p_a = k.ps("p_a", [128, 512])
                p_g = k.ps("p_g", [128, 512])
                p_e = k.ps("p_e", [128, 512])
                p_t = [k.ps("p_t%d" % i, [128, 512]) for i in range(2)]
                p_att = [k.ps("p_att%d" % i, [128, 512]) for i in range(2)]
                p_o = k.ps("p_o", [128, 512])
                it = 0
                for d in range(2):
                    for h in range(4):
                        k.memset("pool", Sst[h][:], 0.0, [Sst[h]])
                    for n in tile_order(d):
                        zt = rot(zts, it)
                        a_T, sp, e_, qk_, ege, os_ = (rot(x, it) for x in (alT, spb, ex, qk, egend, osum))
                        k.load(zt[:], d_z[n * 128:(n + 1) * 128, Z0:Z0 + ZW], zt, src=d_z)
                        k.tr(p_a[0:16, 0:128], zt[:, 512:528], ident[:], [zt, ident], [p_a])
                        k.cp("act", a_T[:], p_a[0:16, 0:128], [p_a], [a_T])
                        k.mm(p_a[:, 128:256], a_T[:], gup[:, d, :], [a_T, gup], [p_a])
                        k.tt("dve", sp[:], p_a[:, 128:256], gbb[:, d, :], ALU.add, [p_a, gbb], [sp])
                        softplus(sp[:], sp[:], [sp], [sp], scale=-1.0)
                        k.mm(p_g[:, 0:128], CUM[d][:], sp[:], [CUM[d], sp], [p_g])
                        k.mm(p_g[:, 128:256], OPP_S[d][:], sp[:], [OPP_S[d], sp], [p_g])
                        k.act(e_[:, 0, :], p_g[:, 0:128], AF.Exp, [p_g], [e_], scale=-1.0 / 16, partial=True)
                        k.act(e_[:, 1, :], p_g[:, 0:128], AF.Exp, [p_g], [e_], scale=1.0 / 16, partial=True)
                        k.act(e_[:, 2, :], p_g[:, 128:256], AF.Exp, [p_g], [e_], scale=-1.0 / 16, partial=True)
                        k.stt("dve", qk_[:, 0, :], zt[:, 0:128], sc, e_[:, 0, :], ALU.mult, ALU.mult, [zt, e_], [qk_],
                              partial=True)
                        k.tt("pool", qk_[:, 1, :], zt[:, 128:256], e_[:, 1, :], ALU.mult, [zt, e_], [qk_], partial=True)
                        k.tt("pool", qk_[:, 2, :], zt[:, 128:256], e_[:, 2, :], ALU.mult, [zt, e_], [qk_], partial=True)
                        for h in range(4):
                            k.mm(p_e[0:32, h:h + 1], sp[:, 32 * h:32 * h + 32], ones[:, 0:1], [sp, ones], [p_e])
                        k.act(ege[:], p_e[0:32, 0:4], AF.Exp, [p_e], [ege], scale=-1.0 / 16)
                        for h in range(4):
                            pt = rot(p_t, h)
                            pa = rot(p_att, h)
                            T_ = qkT[h]
                            A_ = attT[h]
                            k.tr(pt[0:32, 0:128], qk_[:, 0, 32 * h:32 * h + 32], ident[:], [qk_, ident], [pt])
                            k.tr(pt[0:32, 128:256], qk_[:, 1, 32 * h:32 * h + 32], ident[:], [qk_, ident], [pt])
                            k.cp("act", T_[:].rearrange("p a b -> p (a b)"), pt[0:32, 0:256], [pt], [T_])
                            k.mm(pa[:, 0:128], T_[:, 1, :], T_[:, 0, :], [T_], [pa])
                            k.tt("dve", A_[:], pa[:, 0:128], CUM[d][:], ALU.mult, [pa, CUM[d]], [A_])
                            vh = zt[:, 256 + 64 * h:256 + 64 * h + 64]
                            k.mm(p_o[:, 64 * h:64 * h + 64], T_[:, 0, :], Sst[h][:], [T_, Sst[h]], [p_o],
                                 start=True, stop=False)
                            k.mm(p_o[:, 64 * h:64 * h + 64], A_[:], vh, [A_, zt], [p_o], start=False, stop=True)
                            k.mm(pa[0:32, 128:192], qk_[:, 2, 32 * h:32 * h + 32], vh, [qk_, zt], [pa])
                            k.stt("dve", Sst[h][:], Sst[h][:], ege[:, h:h + 1], pa[0:32, 128:192], ALU.mult, ALU.add,
                                  [Sst[h], ege, pa], [Sst[h]])
                        if d == 0:
                            k.cp("act", hsum[:, n, :], p_o[:, 0:256], [p_o], [hsum], partial=True)
                        else:
                            k.tt("dve", os_[:], p_o[:, 0:256], hsum[:, n, :], ALU.add, [p_o, hsum], [os_])
                            head_rms_post(os_, n, nwb, zt[:, 528:784], AF.Silu, 512, rot(tmpb, it), rot(ssb, it),
                                          rot(outb, it), rot(gateb, it), [zt])
                        it += 1
                S.end_phase()

        def phase_mlstm(l):
            Z0 = 0
            ZW = 784
            sc = 32 ** -0.5
            with ExitStack() as st:
                k.st = st
                hsum = k.sb("hsum", [128, NT, 256])
                igb = k.sb("igb", [128, 8])
                fgb = k.sb("fgb", [128, 8])
                nwb = k.sb("nwb", [128, 256])
                k.load(igb[:], prm["ml_ig_b"][l, :].partition_broadcast(128), igb, src=prm["ml_ig_b"])
                k.load(fgb[:], prm["ml_fg_b"][l, :].partition_broadcast(128), fgb, src=prm["ml_fg_b"])
                k.load(nwb[:], prm["ml_norm_w"][l, :].partition_broadcast(128), nwb, src=prm["ml_norm_w"])
                nb = 2
                zts = [k.sb("zm%d" % i, [128, ZW]) for i in range(nb)]
                Sst = [k.sb("Sm%d" % h, [32, 65]) for h in range(4)]
                sp = [k.sb("msp%d" % i, [128, 4]) for i in range(nb)]
                ig = [k.sb("mig%d" % i, [128, 4]) for i in range(nb)]
                Bs = [k.sb("mB%d" % i, [128, 4]) for i in range(nb)]
                eb = [k.sb("meb%d" % i, [128, 4]) for i in range(nb)]
                ek = [k.sb("mek%d" % i, [128, 4]) for i in range(nb)]
                Xb = [k.sb("mX%d" % i, [128, 4, 128]) for i in range(nb)]
                Eb = [k.sb("mE%d" % i, [128, 4, 128]) for i in range(nb)]
                spx = [k.sb("mspx%d" % i, [128, 4, 32]) for i in range(nb)]
                Qd = [k.sb("mQd%d" % i, [128, 4, 32]) for i in range(nb)]
                Ke = [k.sb("mKe%d" % i, [128, 4, 32]) for i in range(nb)]
                vaug = [k.sb("mva%d" % i, [128, 4, 65]) for i in range(nb)]
                ebend = [k.sb("mebe%d" % i, [32, 4]) for i in range(nb)]
                qkT = [k.sb("mqkT%d" % i, [32, 3, 128]) for i in range(4)]
                attT = [k.sb("mattT%d" % i, [128, 128]) for i in range(4)]
                den = [k.sb("mden%d" % i, [128, 4]) for i in range(nb)]
                osum = [k.sb("osum%d" % i, [128, 256]) for i in range(nb)]
                tmpb = [k.sb("tmpb%d" % i, [128, 256]) for i in range(nb)]
                ssb = [k.sb("ssb%d" % i, [128, 4]) for i in range(nb)]
                outb = [k.sb("outb%d" % i, [128, 256]) for i in range(nb)]
                gateb = [k.sb("gateb%d" % i, [128, 256]) for i in range(nb)]
                p_b = k.ps("p_b", [128, 512])
                p_row = k.ps("p_row", [128, 512])
                p_t = [k.ps("p_t%d" % i, [128, 512]) for i in range(2)]
                p_att = [k.ps("p_att%d" % i, [128, 512]) for i in range(2)]
                p_o = k.ps("p_o", [128, 512])
                for v_ in vaug:
                    k.memset("pool", v_[:], 1.0, [v_])
                it = 0
                for d in range(2):
                    for h in range(4):
                        k.memset("pool", Sst[h][:], 0.0, [Sst[h]])
                    for n in tile_order(d):
                        zt, sp_, ig_, B_, eb_, ek_, X_, E_, spx_, Qd_, Ke_, va_, ebe_, den_, os_ = (
                            rot(x, it) for x in (zts, sp, ig, Bs, eb, ek, Xb, Eb, spx, Qd, Ke, vaug, ebend, den, osum))
                        k.load(zt[:], d_z[n * 128:(n + 1) * 128, Z0:Z0 + ZW], zt, src=d_z)
                        k.tt("dve", sp_[:], zt[:, 776 + 4 * d:780 + 4 * d], fgb[:, 4 * d:4 * d + 4], ALU.add, [zt, fgb],
                             [sp_])
                        softplus(sp_[:], sp_[:], [sp_], [sp_], scale=-1.0)
                        k.tt("dve", ig_[:], zt[:, 768 + 4 * d:772 + 4 * d], igb[:, 4 * d:4 * d + 4], ALU.add, [zt, igb],
                             [ig_])
                        k.mm(p_b[:, 0:4], CUM[d][:], sp_[:], [CUM[d], sp_], [p_b])
                        k.mm(p_b[:, 4:8], OPP_S[d][:], sp_[:], [OPP_S[d], sp_], [p_b])
                        k.cp("dve", B_[:], p_b[:, 0:4], [p_b], [B_])
                        k.act(eb_[:], p_b[:, 0:4], AF.Exp, [p_b], [eb_], scale=-1.0)
                        k.tt("dve", ek_[:], ig_[:], p_b[:, 4:8], ALU.subtract, [ig_, p_b], [ek_])
                        k.act(ek_[:], ek_[:], AF.Exp, [ek_], [ek_])
                        k.tt("pool", X_[:], ident[:].unsqueeze(1).to_broadcast([128, 4, 128]),
                             B_[:].unsqueeze(2).to_broadcast([128, 4, 128]), ALU.mult, [ident, B_], [X_])
                        k.mm(p_row[:], ones[:], X_[:].rearrange("p a b -> p (a b)"), [ones, X_], [p_row])
                        k.stt("dve", E_[:], p_row[:].rearrange("p (a b) -> p a b", a=4), -1.0,
                              B_[:].unsqueeze(2).to_broadcast([128, 4, 128]), ALU.mult, ALU.add, [p_row, B_], [E_])
                        k.stt("dve", E_[:], E_[:], 0.0, ig_[:].unsqueeze(2).to_broadcast([128, 4, 128]), ALU.min, ALU.add,
                              [E_, ig_], [E_])
                        k.act(E_[:], E_[:], AF.Exp, [E_], [E_])
                        k.tt("pool", E_[:], E_[:], CUM[d][:].unsqueeze(1).to_broadcast([128, 4, 128]), ALU.mult,
                             [E_, CUM[d]], [E_])
                        k.stt("dve", Qd_[:], zt[:, 0:128].rearrange("p (h d) -> p h d", h=4), sc,
                              eb_[:].unsqueeze(2).to_broadcast([128, 4, 32]), ALU.mult, ALU.mult, [zt, eb_], [Qd_])
                        k.tt("pool", Ke_[:], zt[:, 128:256].rearrange("p (h d) -> p h d", h=4),
                             ek_[:].unsqueeze(2).to_broadcast([128, 4, 32]), ALU.mult, [zt, ek_], [Ke_])
                        k.cp("pool", spx_[:], sp_[:].unsqueeze(2).to_broadcast([128, 4, 32]), [sp_], [spx_])
                        k.cp("pool", va_[:, :, 0:64], zt[:, 256:512].rearrange("p (h d) -> p h d", h=4), [zt], [va_])
                        for h in range(4):
                            k.mm(p_b[0:32, 8 + h:9 + h], spx_[:, h, :], ones[:, 0:1], [spx_, ones], [p_b])
                        k.act(ebe_[:], p_b[0:32, 8:12], AF.Exp, [p_b], [ebe_], scale=-1.0)
                        for h in range(4):
                            pt = rot(p_t, h)
                            pa = rot(p_att, h)
                            T_ = qkT[h]
                            A_ = attT[h]
                            k.tr(pt[0:32, 0:128], zt[:, 32 * h:32 * h + 32], ident[:], [zt, ident], [pt])
                            k.tr(pt[0:32, 128:256], zt[:, 128 + 32 * h:128 + 32 * h + 32], ident[:], [zt, ident], [pt])
                            k.tr(pt[0:32, 256:384], Qd_[:, h, :], ident[:], [Qd_, ident], [pt])
                            k.cp("act", T_[:].rearrange("p a b -> p (a b)"), pt[0:32, 0:384], [pt], [T_])
                            k.mm(pa[:, 0:128], T_[:, 1, :], T_[:, 0, :], [T_], [pa])
                            k.stt("dve", A_[:], pa[:, 0:128], sc, E_[:, h, :], ALU.mult, ALU.mult, [pa, E_], [A_])
                            k.mm(p_o[:, 65 * h:65 * h + 65], T_[:, 2, :], Sst[h][:], [T_, Sst[h]], [p_o],
                                 start=True, stop=False)
                            k.mm(p_o[:, 65 * h:65 * h + 65], A_[:], va_[:, h, :], [A_, va_], [p_o], start=False,
                                 stop=True)
                            k.mm(pa[0:32, 128:193], Ke_[:, h, :], va_[:, h, :], [Ke_, va_], [pa])
                            k.stt("dve", Sst[h][:], Sst[h][:], ebe_[:, h:h + 1], pa[0:32, 128:193], ALU.mult, ALU.add,
                                  [Sst[h], ebe_, pa], [Sst[h]])
                        po3 = p_o[:, 0:260].rearrange("p (h v) -> p h v", h=4)
                        k.act(den_[:].unsqueeze(2), po3[:, :, 64:65], AF.Abs, [p_o], [den_])
                        k.ts("dve", den_[:], den_[:], 1.0, ALU.max, [den_], [den_])
                        k.recip(den_[:], den_[:], [den_], [den_])
                        if d == 0:
                            k.tt("dve", hsum[:, n, :].rearrange("p (h v) -> p h v", h=4), po3[:, :, 0:64],
                                 den_[:].unsqueeze(2).to_broadcast([128, 4, 64]), ALU.mult, [p_o, den_], [hsum],
                                 partial=True)
                        else:
                            k.tt("dve", os_[:].rearrange("p (h v) -> p h v", h=4), po3[:, :, 0:64],
                                 den_[:].unsqueeze(2).to_broadcast([128, 4, 64]), ALU.mult, [p_o, den_], [os_])
                            k.tt("pool", os_[:], os_[:], hsum[:, n, :], ALU.add, [os_, hsum], [os_])
                            head_rms_post(os_, n, nwb, zt[:, 512:768], AF.Sigmoid, 0, rot(tmpb, it), rot(ssb, it),
                                          rot(outb, it), rot(gateb, it), [zt])
                        it += 1
                S.end_phase()

        def phase_gdn_conv(l):
            Z0 = 784 + 1024 + 784
            with ExitStack() as st:
                k.st = st
                wcb = k.sb("wcb", [128, 9, 768])
                k.load(wcb[:].rearrange("p a b -> p (a b)"),
                       prm["gd_conv_w"][l, :, :].rearrange("a b -> (a b)").partition_broadcast(128), wcb,
                       src=prm["gd_conv_w"])
                mcol = k.sb("mcol", [128, 2])
                k.tt("dve", mcol[:, 0:1], ident[:, 0:1], ident[:, 64:65], ALU.add, [ident], [mcol], partial=True)
                k.tt("dve", mcol[:, 1:2], ident[:, 63:64], ident[:, 127:128], ALU.add, [ident], [mcol], partial=True)
                k.ts("dve", mcol[:], mcol[:], -1.0, ALU.mult, [mcol], [mcol], s2=1.0, op1=ALU.add)
                sh = [k.sb("sh%d" % i, [128, 768]) for i in range(4)]
                acc = [k.sb("acc%d" % i, [128, 768]) for i in range(2)]
                tmp = [k.sb("ctmp%d" % i, [128, 768]) for i in range(2)]
                ss = [k.sb("css%d" % i, [128, 8]) for i in range(2)]
                si = 0
                for n in range(NT):
                    t0 = n * 128
                    isctx = n < NCT
                    seg0, seg1 = (0, NCTX) if isctx else (NCTX, L)
                    a_ = rot(acc, n)
                    first = True
                    for dr in ((0,) if isctx else (-1, 0, 1)):
                        for dc in (-1, 0, 1):
                            tap = (dr + 1) * 3 + (dc + 1)
                            lo, hi = 0, 128
                            for p in range(128):
                                pass
                            ps_ = [p for p in range(128)
                                   if seg0 <= t0 + p + 64 * dr < seg1 and seg0 <= t0 + p + 64 * dr + dc < seg1]
                            if not ps_:
                                continue
                            lo, hi = ps_[0], ps_[-1] + 1
                            s_ = rot(sh, si)
                            si += 1
                            if lo > 0 or hi < 128:
                                k.memset("pool", s_[:], 0.0, [s_])
                            src0 = t0 + lo + 64 * dr + dc
                            k.load(s_[lo:hi, :], d_z[src0:src0 + (hi - lo), Z0:Z0 + 768], s_, src=d_z, partial=True)
                            if dc == 0 or isctx:
                                msk = 1.0
                                rds = [s_, wcb]
                            else:
                                msk = mcol[:, 0:1] if dc == -1 else mcol[:, 1:2]
                                rds = [s_, wcb, mcol]
                            if first:
                                k.stt("dve", a_[:], s_[:], msk, wcb[:, tap, :], ALU.mult, ALU.mult, rds, [a_])
                                first = False
                            else:
                                t_ = rot(tmp, si)
                                k.stt("dve", t_[:], s_[:], msk, wcb[:, tap, :], ALU.mult, ALU.mult, rds, [t_])
                                k.tt("pool", a_[:], a_[:], t_[:], ALU.add, [a_, t_], [a_])
                    k.act(a_[:], a_[:], AF.Silu, [a_], [a_])
                    t_ = rot(tmp, n)
                    s8 = rot(ss, n)
                    k.tt("pool", t_[:, 0:512], a_[:, 0:512], a_[:, 0:512], ALU.mult, [a_], [t_])
                    k.red("dve", s8[:], t_[:, 0:512].rearrange("p (h d) -> p h d", h=8), [t_], [s8])
                    k.rsqrt(s8[:], s8[:], [s8], [s8], scale=1.0, eps=EPS)
                    k.tt("dve", a_[:, 0:512].rearrange("p (h d) -> p h d", h=8),
                         a_[:, 0:512].rearrange("p (h d) -> p h d", h=8),
                         s8[:].unsqueeze(2).to_broadcast([128, 8, 64]), ALU.mult, [a_, s8], [a_])
                    k.store(d_qkvc[t0:t0 + 128, :], a_[:], a_, d_qkvc)
                S.end_phase()

        def phase_gdn(l):
            Z0 = 784 + 1024 + 784
            sc = 64 ** -0.5
            with ExitStack() as st:
                k.st = st
                hsum = k.sb("hsum", [128, NT, 256])
                alb = k.sb("alb", [128, 8])
                dtb = k.sb("dtb", [128, 8])
                nwb = k.sb("nwb", [128, 256])
                k.load(alb[:], prm["gd_a_log"][l, :].partition_broadcast(128), alb, src=prm["gd_a_log"])
                k.load(dtb[:], prm["gd_dt_bias"][l, :].partition_broadcast(128), dtb, src=prm["gd_dt_bias"])
                k.load(nwb[:], prm["gd_norm_w"][l, :].partition_broadcast(128), nwb, src=prm["gd_norm_w"])
                k.act(alb[:], alb[:], AF.Exp, [alb], [alb])
                nb = 2
                qks = [k.sb("gqk%d" % i, [128, 768]) for i in range(nb)]
                zss = [k.sb("gzs%d" % i, [128, 272]) for i in range(nb)]
                Sst = [k.sb("Sd%d" % h, [64, 64]) for h in range(4)]
                beta = [k.sb("gbeta%d" % i, [128, 4]) for i in range(nb)]
                nbeta = [k.sb("gnbeta%d" % i, [128, 4]) for i in range(nb)]
                la = [k.sb("gla%d" % i, [128, 4]) for i in range(nb)]
                Gs = [k.sb("gG%d" % i, [128, 4]) for i in range(nb)]
                eG = [k.sb("geG%d" % i, [128, 4]) for i in range(nb)]
                bg = [k.sb("gbg%d" % i, [128, 4]) for i in range(nb)]
                erem = [k.sb("gerem%d" % i, [128, 4]) for i in range(nb)]
                Xb = [k.sb("gX%d" % i, [128, 4, 128]) for i in range(nb)]
                E1 = [k.sb("gE1%d" % i, [128, 4, 128]) for i in range(nb)]
                E2 = [k.sb("gE2%d" % i, [128, 4, 128]) for i in range(nb)]
                lax = [k.sb("glax%d" % i, [128, 4, 64]) for i in range(nb)]
                egend = [k.sb("gege%d" % i, [64, 4]) for i in range(nb)]
                Qg = [k.sb("gQg%d" % i, [128, 256]) for i in range(nb)]
                Ke = [k.sb("gKe%d" % i, [128, 256]) for i in range(nb)]
                FT = [k.sb("gFT%d" % h, [64, 3, 128]) for h in range(4)]
                attT = [k.sb("gattT%d" % h, [128, 128]) for h in range(4)]
                Nb = [k.sb("gN%d" % h, [128, 128]) for h in range(4)]
                NTb = [k.sb("gNT%d" % h, [128, 128]) for h in range(4)]
                N2b = [k.sb("gN2%d" % h, [128, 128]) for h in range(4)]
                NT2b = [k.sb("gNT2%d" % h, [128, 128]) for h in range(4)]
                Xs = [k.sb("gXs%d" % h, [128, 128]) for h in range(4)]
                TTb = [k.sb("gTT%d" % h, [64, 64]) for h in range(4)]
                Qe = [k.sb("gQe%d" % h, [64, 128]) for h in range(4)]
                osum = [k.sb("osum%d" % i, [128, 256]) for i in range(nb)]
                tmpb = [k.sb("tmpb%d" % i, [128, 256]) for i in range(nb)]
                ssb = [k.sb("ssb%d" % i, [128, 4]) for i in range(nb)]
                outb = [k.sb("outb%d" % i, [128, 256]) for i in range(nb)]
                gateb = [k.sb("gateb%d" % i, [128, 256]) for i in range(nb)]
                p_b = k.ps("p_b", [128, 512])
                p_row = k.ps("p_row", [128, 512])
                p_h = [k.ps("p_h%d" % i, [128, 512]) for i in range(4)]
                p_o = k.ps("p_o", [128, 512])
                p_s = k.ps("p_s", [128, 512])
                it = 0
                for d in range(2):
                    for h in range(4):
                        k.memset("pool", Sst[h][:], 0.0, [Sst[h]])
                    for n in tile_order(d):
                        qk_, zs, be_, la_, G_, eG_, bg_, er_, X_, E1_, E2_, lax_, ege, Qg_, Ke_, os_ = (
                            rot(x, it) for x in (qks, zss, beta, la, Gs, eG, bg, erem, Xb, E1, E2, lax, egend, Qg, Ke,
                                                 osum))
                        k.load(qk_[:], d_qkvc[n * 128:(n + 1) * 128, :], qk_, src=d_qkvc)
                        k.load(zs[:], d_z[n * 128:(n + 1) * 128, Z0 + 768:Z0 + 1040], zs, src=d_z)
                        k.act(be_[:], zs[:, 256 + 4 * d:260 + 4 * d], AF.Sigmoid, [zs], [be_])
                        nbe_ = rot(nbeta, it)
                        k.ts("dve", nbe_[:], be_[:], -1.0, ALU.mult, [be_], [nbe_])
                        k.tt("dve", la_[:], zs[:, 264 + 4 * d:268 + 4 * d], dtb[:, 4 * d:4 * d + 4], ALU.add, [zs, dtb],
                             [la_])
                        softplus(la_[:], la_[:], [la_], [la_], scale=1.0)
                        k.tt("dve", la_[:], la_[:], alb[:, 4 * d:4 * d + 4], ALU.mult, [la_, alb], [la_])
                        k.mm(p_b[:, 0:4], CUM[d][:], la_[:], [CUM[d], la_], [p_b])
                        k.mm(p_b[:, 4:8], OPP_S[d][:], la_[:], [OPP_S[d], la_], [p_b])
                        k.cp("dve", G_[:], p_b[:, 0:4], [p_b], [G_])
                        k.act(eG_[:], p_b[:, 0:4], AF.Exp, [p_b], [eG_], scale=-1.0)
                        k.act(er_[:], p_b[:, 4:8], AF.Exp, [p_b], [er_], scale=-1.0)
                        k.tt("dve", bg_[:], be_[:], eG_[:], ALU.mult, [be_, eG_], [bg_])
                        k.tt("pool", X_[:], ident[:].unsqueeze(1).to_broadcast([128, 4, 128]),
                             G_[:].unsqueeze(2).to_broadcast([128, 4, 128]), ALU.mult, [ident, G_], [X_])
                        k.mm(p_row[:], ones[:], X_[:].rearrange("p a b -> p (a b)"), [ones, X_], [p_row])
                        k.stt("dve", E1_[:], p_row[:].rearrange("p (a b) -> p a b", a=4), -1.0,
                              G_[:].unsqueeze(2).to_broadcast([128, 4, 128]), ALU.mult, ALU.add, [p_row, G_], [E1_])
                        k.ts("dve", E2_[:], E1_[:], -1.0, ALU.mult, [E1_], [E2_], s2=0.0, op1=ALU.min)
                        k.ts("dve", E1_[:], E1_[:], 0.0, ALU.min, [E1_], [E1_])
                        k.act(E1_[:], E1_[:], AF.Exp, [E1_], [E1_])
                        k.act(E2_[:], E2_[:], AF.Exp, [E2_], [E2_])
                        k.tt("pool", E1_[:], E1_[:], CUM[d][:].unsqueeze(1).to_broadcast([128, 4, 128]), ALU.mult,
                             [E1_, CUM[d]], [E1_])
                        k.tt("pool", E2_[:], E2_[:], OPP_S[d][:].unsqueeze(1).to_broadcast([128, 4, 128]), ALU.mult,
                             [E2_, OPP_S[d]], [E2_])
                        k.stt("dve", Qg_[:].rearrange("p (h d) -> p h d", h=4),
                              qk_[:, 0:256].rearrange("p (h d) -> p h d", h=4), sc,
                              eG_[:].unsqueeze(2).to_broadcast([128, 4, 64]), ALU.mult, ALU.mult, [qk_, eG_], [Qg_])
                        k.tt("pool", Ke_[:].rearrange("p (h d) -> p h d", h=4),
                             qk_[:, 256:512].rearrange("p (h d) -> p h d", h=4),
                             er_[:].unsqueeze(2).to_broadcast([128, 4, 64]), ALU.mult, [qk_, er_], [Ke_])
                        k.cp("pool", lax_[:], la_[:].unsqueeze(2).to_broadcast([128, 4, 64]), [la_], [lax_])
                        for h in range(4):
                            k.mm(p_b[0:64, 8 + h:9 + h], lax_[:, h, :], ones[:, 0:1], [lax_, ones], [p_b])
                        k.act(ege[:], p_b[0:64, 8:12], AF.Exp, [p_b], [ege], scale=-1.0)
                        if CUT <= 1:
                            it += 1
                            continue
                        for h in range(4):
                            ph = p_h[h]
                            k.tr(ph[0:64, 0:128], qk_[:, 256 + 64 * h:320 + 64 * h], ident[:], [qk_, ident], [ph])
                            k.tr(ph[0:64, 128:256], qk_[:, 64 * h:64 * h + 64], ident[:], [qk_, ident], [ph])
                            k.tr(ph[0:64, 256:384], Qg_[:, 64 * h:64 * h + 64], ident[:], [Qg_, ident], [ph])
                            k.cp("act", FT[h][:].rearrange("p a b -> p (a b)"), ph[0:64, 0:384], [ph], [FT[h]])
                        for h in range(4):
                            ph = p_h[h]
                            k.mm(ph[:, 0:256], FT[h][:, 0, :], FT[h][:, 0:2, :].rearrange("p a b -> p (a b)"),
                                 [FT[h]], [ph])
                            k.stt("dve", attT[h][:], ph[:, 128:256], sc, E1_[:, h, :], ALU.mult, ALU.mult, [ph, E1_],
                                  [attT[h]])
                            k.stt("dve", Nb[h][:], ph[:, 0:128], nbe_[:, h:h + 1], E2_[:, h, :], ALU.mult, ALU.mult,
                                  [ph, nbe_, E2_], [Nb[h]])
                            k.ts("dve", Xs[h][:, 0:64], qk_[:, 512 + 64 * h:576 + 64 * h], be_[:, h:h + 1], ALU.mult,
                                 [qk_, be_], [Xs[h]], partial=True)
                            k.ts("dve", Xs[h][:, 64:128], qk_[:, 256 + 64 * h:320 + 64 * h], bg_[:, h:h + 1], ALU.mult,
                                 [qk_, bg_], [Xs[h]], partial=True)
                        for h in range(4):
                            ph = p_h[h]
                            k.tr(ph[:, 256:384], Nb[h][:], ident[:], [Nb[h], ident], [ph])
                            k.cp("act", NTb[h][:], ph[:, 256:384], [ph], [NTb[h]])
                        if CUT <= 2:
                            it += 1
                            continue
                        solve_views(Xs, Nb, NTb, N2b, NT2b, p_h)
                        if CUT <= 3:
                            it += 1
                            continue
                        for h in range(4):
                            ph = p_h[h]
                            u = Xs[h][:, 0:64]
                            w = Xs[h][:, 64:128]
                            Keh = Ke_[:, 64 * h:64 * h + 64]
                            k.mm(ph[0:64, 0:64], w, Keh, [Xs[h], Ke_], [ph])
                            k.stt("dve", TTb[h][:], ident[0:64, 0:64], ege[:, h:h + 1], ph[0:64, 0:64], ALU.mult,
                                  ALU.subtract, [ident, ege, ph], [TTb[h]])
                            k.mm(ph[0:64, 128:256], w, attT[h][:], [Xs[h], attT[h]], [ph])
                            k.tt("dve", Qe[h][:], FT[h][:, 2, :], ph[0:64, 128:256], ALU.subtract, [FT[h], ph], [Qe[h]])
                            k.mm(p_o[:, 64 * h:64 * h + 64], Qe[h][:], Sst[h][:], [Qe[h], Sst[h]], [p_o], start=True,
                                 stop=False)
                            k.mm(p_o[:, 64 * h:64 * h + 64], attT[h][:], u, [attT[h], Xs[h]], [p_o], start=False,
                                 stop=True)
                            k.mm(p_s[0:64, 64 * h:64 * h + 64], TTb[h][:], Sst[h][:], [TTb[h], Sst[h]], [p_s],
                                 start=True, stop=False)
                            k.mm(p_s[0:64, 64 * h:64 * h + 64], Keh, u, [Ke_, Xs[h]], [p_s], start=False, stop=True)
                            k.cp("act", Sst[h][:], p_s[0:64, 64 * h:64 * h + 64], [p_s], [Sst[h]])
                        if d == 0:
                            k.cp("act", hsum[:, n, :], p_o[:, 0:256], [p_o], [hsum], partial=True)
                        else:
                            k.tt("dve", os_[:], p_o[:, 0:256], hsum[:, n, :], ALU.add, [p_o, hsum], [os_])
                            head_rms_post(os_, n, nwb, zs[:, 0:256], AF.Silu, 768, rot(tmpb, it), rot(ssb, it),
                                          rot(outb, it), rot(gateb, it), [zs])
                        it += 1
                S.end_phase()

        def solve_views(Xs, Ns, NTs, N2s, NT2s, p_h):
            nh = len(Xs)
            cur = [(Ns[i], NTs[i]) for i in range(nh)]
            nxt = [(N2s[i], NT2s[i]) for i in range(nh)]
            for step in range(7):
                for i in range(nh):
                    N_, NT_ = cur[i]
                    ph = p_h[i]
                    k.mm(ph[:, 0:128], NT_[:], Xs[i][:], [NT_, Xs[i]], [ph])
                    if step < 6:
                        k.mm(ph[:, 128:256], NT_[:], N_[:], [NT_, N_], [ph])
                        k.mm(ph[:, 256:384], N_[:], NT_[:], [N_, NT_], [ph])
                for i in range(nh):
                    ph = p_h[i]
                    k.tt("dve", Xs[i][:], Xs[i][:], ph[:, 0:128], ALU.add, [Xs[i], ph], [Xs[i]])
                    if step < 6:
                        k.cp("dve", nxt[i][0][:], ph[:, 128:256], [ph], [nxt[i][0]])
                        k.cp("dve", nxt[i][1][:], ph[:, 256:384], [ph], [nxt[i][1]])
                cur, nxt = nxt, cur

        RW_R, RW_V, RW_KK, RW_G, RW_BON = 0, 256, 512, 768, 1024
        RW_D0 = 1280

        def phase_rwkv_pre(l):
            Z0 = 784
            with ExitStack() as st:
                k.st = st
                coef = k.sb("coef", [128, 3, 1024])
                k.load(coef[:, 1, :], prm["rw_mu_prev"][l, :].partition_broadcast(128), coef, src=prm["rw_mu_prev"],
                       partial=True)
                k.load(coef[:, 2, :], prm["rw_mu_next"][l, :].partition_broadcast(128), coef, src=prm["rw_mu_next"],
                       partial=True)
                k.tt("dve", coef[:, 0, :], coef[:, 1, :], coef[:, 2, :], ALU.add, [coef], [coef])
                k.ts("dve", coef[:, 0, :], coef[:, 0, :], -1.0, ALU.mult, [coef], [coef], s2=1.0, op1=ALU.add)
                vecs = k.sb("rvecs", [128, 9, 256])
                for i, (nm, idx) in enumerate((("rw_k_k", None), ("rw_k_a", None), ("rw_r_k", None), ("rw_w0", 0),
                                               ("rw_w0", 1), ("rw_a0", 0), ("rw_a0", 1))):
                    src = prm[nm][l, :] if idx is None else prm[nm][l, idx, :]
                    k.load(vecs[:, i, :], src.partition_broadcast(128), vecs, src=prm[nm], partial=True)
                k.ts("dve", vecs[:, 7, :], vecs[:, 1, :], -1.0, ALU.mult, [vecs], [vecs], s2=1.0, op1=ALU.add)
                wup = k.sb("wup", [64, 2, 256])
                aup = k.sb("aup", [64, 2, 256])
                gupw = k.sb("gupw", [128, 256])
                for d in range(2):
                    k.load(wup[:, d, :], prm["rw_w_up"][l, d, :, :], wup, src=prm["rw_w_up"], partial=True)
                    k.load(aup[:, d, :], prm["rw_a_up"][l, d, :, :], aup, src=prm["rw_a_up"], partial=True)
                k.load(gupw[:], prm["rw_g_up"][l, :, :], gupw, src=prm["rw_g_up"])
                nb = 2
                z0 = [k.sb("rz0_%d" % i, [128, 1024]) for i in range(nb)]
                zp = [k.sb("rzp_%d" % i, [128, 1024]) for i in range(nb)]
                zn = [k.sb("rzn_%d" % i, [128, 1024]) for i in range(nb)]
                zm = [k.sb("rzm_%d" % i, [128, 1024]) for i in range(nb)]
                ob = [k.sb("rob_%d" % i, [128, RWP_W]) for i in range(nb)]
                tmp = [k.sb("rtmp_%d" % i, [128, 256]) for i in range(nb)]
                tmp2 = [k.sb("rtmp2_%d" % i, [128, 256]) for i in range(nb)]
                s4 = [k.sb("rs4_%d" % i, [128, 4]) for i in range(nb)]
                lT = [k.sb("rlT_%d" % i, [128, 384]) for i in range(nb)]
                act_in = [k.sb("ract_%d" % i, [128, 256]) for i in range(nb)]
                p_t = k.ps("rp_t", [128, 512])
                p_m = [k.ps("rp_m%d" % i, [128, 512]) for i in range(2)]
                for n in range(NT):
                    t0 = n * 128
                    z0_, zp_, zn_, zm_, ob_, tmp_, tmp2_, s4_, lT_, ai_ = (
                        rot(x, n) for x in (z0, zp, zn, zm, ob, tmp, tmp2, s4, lT, act_in))
                    seg0, seg1 = (0, NCTX) if n < NCT else (NCTX, L)
                    k.load(z0_[:], d_z[t0:t0 + 128, Z0:Z0 + 1024], z0_, src=d_z)
                    if t0 == seg0:
                        k.memset("pool", zp_[:], 0.0, [zp_])
                        k.load(zp_[1:128, :], d_z[t0:t0 + 127, Z0:Z0 + 1024], zp_, src=d_z, partial=True)
                    else:
                        k.load(zp_[:], d_z[t0 - 1:t0 + 127, Z0:Z0 + 1024], zp_, src=d_z)
                    if t0 + 128 == seg1:
                        k.memset("pool", zn_[:], 0.0, [zn_])
                        k.load(zn_[0:127, :], d_z[t0 + 1:t0 + 128, Z0:Z0 + 1024], zn_, src=d_z, partial=True)
                    else:
                        k.load(zn_[:], d_z[t0 + 1:t0 + 129, Z0:Z0 + 1024], zn_, src=d_z)
                    k.tt("dve", zm_[:], z0_[:], coef[:, 0, :], ALU.mult, [z0_, coef], [zm_])
                    k.tt("pool", zp_[:], zp_[:], coef[:, 1, :], ALU.mult, [zp_, coef], [zp_])
                    k.tt("pool", zn_[:], zn_[:], coef[:, 2, :], ALU.mult, [zn_, coef], [zn_])
                    k.tt("dve", zm_[:], zm_[:], zp_[:], ALU.add, [zm_, zp_], [zm_])
                    k.tt("dve", zm_[:], zm_[:], zn_[:], ALU.add, [zm_, zn_], [zm_])
                    r_ = zm_[:, 0:256]
                    k_ = zm_[:, 256:512]
                    v_ = zm_[:, 512:768]
                    k.cp("pool", ob_[:, RW_R:RW_R + 256], r_, [zm_], [ob_], partial=True)
                    k.cp("pool", ob_[:, RW_V:RW_V + 256], v_, [zm_], [ob_], partial=True)
                    k.tt("dve", tmp_[:], k_, vecs[:, 0, :], ALU.mult, [zm_, vecs], [tmp_])
                    k.tt("pool", tmp2_[:], tmp_[:], tmp_[:], ALU.mult, [tmp_], [tmp2_])
                    k.red("dve", s4_[:], tmp2_[:].rearrange("p (h d) -> p h d", h=4), [tmp2_], [s4_])
                    k.rsqrt(s4_[:], s4_[:], [s4_], [s4_], scale=1.0, eps=EPS)
                    k.tt("dve", ob_[:, RW_KK:RW_KK + 256].rearrange("p (h d) -> p h d", h=4),
                         tmp_[:].rearrange("p (h d) -> p h d", h=4), s4_[:].unsqueeze(2).to_broadcast([128, 4, 64]),
                         ALU.mult, [tmp_, s4_], [ob_], partial=True)
                    k.act(ai_[:, 0:64], zm_[:, 768:832], AF.Tanh, [zm_], [ai_], partial=True)
                    k.act(ai_[:, 128:256], zm_[:, 896:1024], AF.Sigmoid, [zm_], [ai_], partial=True)
                    k.tr(p_t[0:64, 0:128], ai_[:, 0:64], ident[:], [ai_, ident], [p_t])
                    k.tr(p_t[0:64, 128:256], zm_[:, 832:896], ident[:], [zm_, ident], [p_t])
                    k.tr(p_t[:, 256:384], ai_[:, 128:256], ident[:], [ai_, ident], [p_t])
                    k.cp("act", lT_[0:64, 0:256], p_t[0:64, 0:256], [p_t], [lT_], partial=True)
                    k.cp("act", lT_[:, 256:384], p_t[:, 256:384], [p_t], [lT_], partial=True)
                    pm = rot(p_m, n)
                    k.mm(pm[:, 0:256], lT_[:, 256:384], gupw[:], [lT_, gupw], [pm])
                    k.cp("act", ob_[:, RW_G:RW_G + 256], pm[:, 0:256], [pm], [ob_], partial=True)
                    pm2 = rot(p_m, n + 1)
                    for d in range(2):
                        c0 = RW_D0 + d * 768
                        k.mm(pm2[:, d * 256:(d + 1) * 256], lT_[0:64, 0:128], wup[:, d, :], [lT_, wup], [pm2])
                        k.tt("dve", tmp_[:], pm2[:, d * 256:(d + 1) * 256], vecs[:, 3 + d, :], ALU.add, [pm2, vecs],
                             [tmp_])
                        k.act(ob_[:, c0:c0 + 256], tmp_[:], AF.Sigmoid, [tmp_], [ob_], partial=True)
                    for d in range(2):
                        c0 = RW_D0 + d * 768
                        k.mm(pm[:, 256:512], lT_[0:64, 128:256], aup[:, d, :], [lT_, aup], [pm])
                        k.tt("dve", tmp_[:], pm[:, 256:512], vecs[:, 5 + d, :], ALU.add, [pm, vecs], [tmp_])
                        k.act(tmp_[:], tmp_[:], AF.Sigmoid, [tmp_], [tmp_])
                        k.tt("pool", ob_[:, c0 + 512:c0 + 768], ob_[:, RW_KK:RW_KK + 256], tmp_[:], ALU.mult,
                             [ob_, tmp_], [ob_], partial=True)
                        k.tt("dve", tmp2_[:], tmp_[:], vecs[:, 1, :], ALU.mult, [tmp_, vecs], [tmp2_])
                        k.tt("dve", tmp2_[:], tmp2_[:], vecs[:, 7, :], ALU.add, [tmp2_, vecs], [tmp2_])
                        k.tt("dve", ob_[:, c0 + 256:c0 + 512], k_, tmp2_[:], ALU.mult, [zm_, tmp2_], [ob_],
                             partial=True)
                    k.tt("dve", tmp_[:], ob_[:, RW_D0 + 256:RW_D0 + 512], ob_[:, RW_D0 + 768 + 256:RW_D0 + 768 + 512],
                         ALU.add, [ob_], [tmp_])
                    k.tt("dve", tmp_[:], tmp_[:], r_, ALU.mult, [tmp_, zm_], [tmp_])
                    k.tt("dve", tmp_[:], tmp_[:], vecs[:, 2, :], ALU.mult, [tmp_, vecs], [tmp_])
                    k.red("dve", s4_[:], tmp_[:].rearrange("p (h d) -> p h d", h=4), [tmp_], [s4_])
                    k.tt("dve", ob_[:, RW_BON:RW_BON + 256].rearrange("p (h d) -> p h d", h=4),
                         v_.rearrange("p (h d) -> p h d", h=4), s4_[:].unsqueeze(2).to_broadcast([128, 4, 64]),
                         ALU.mult, [zm_, s4_], [ob_], partial=True)
                    k.store(d_rwp[t0:t0 + 128, :], ob_[:], ob_, d_rwp)
                S.end_phase()

        def phase_rwkv(l):
            with ExitStack() as st:
                k.st = st
                hsum = k.sb("hsum", [128, NT, 256])
                gnw = k.sb("gnw", [128, 256])
                gnb = k.sb("gnb", [128, 256])
                k.load(gnw[:], prm["rw_gn_w"][l, :].partition_broadcast(128), gnw, src=prm["rw_gn_w"])
                k.load(gnb[:], prm["rw_gn_b"][l, :].partition_broadcast(128), gnb, src=prm["rw_gn_b"])
                mask2 = [k.sb("rmask%d" % d, [128, 256]) for d in range(2)]
                for d in range(2):
                    k.cp("pool", mask2[d][:, 0:128], CUMS[d][:], [CUMS[d]], [mask2[d]], partial=True)
                    k.cp("pool", mask2[d][:, 128:256], CUM[d][:], [CUM[d]], [mask2[d]], partial=True)
                nb = 2
                cm = [k.sb("rcm%d" % i, [128, 768]) for i in range(nb)]
                dd = [k.sb("rdd%d" % i, [128, 768]) for i in range(nb)]
                gb = [k.sb("rgb%d" % i, [128, 512]) for i in range(nb)]
                ex = [k.sb("rex%d" % i, [128, 4, 256]) for i in range(nb)]
                ops = [k.sb("rops%d" % i, [128, 6, 256]) for i in range(nb)]
                egend = [k.sb("rege%d" % i, [64, 4]) for i in range(nb)]
                Sst = [k.sb("Sr%d" % h, [64, 64]) for h in range(4)]
                FT = [k.sb("rFT%d" % h, [64, 4, 128]) for h in range(4)]
                MT = [k.sb("rMT%d" % h, [128, 2, 256]) for h in range(4)]
                Nb = [k.sb("rN%d" % h, [128, 128]) for h in range(4)]
                N2b = [k.sb("rN2%d" % h, [128, 128]) for h in range(4)]
                NT2b = [k.sb("rNT2%d" % h, [128, 128]) for h in range(4)]
                NT0 = [k.sb("rNT0%d" % h, [128, 128]) for h in range(4)]
                Xs = [k.sb("rXs%d" % h, [128, 128]) for h in range(4)]
                TTb = [k.sb("rTT%d" % h, [64, 64]) for h in range(4)]
                Qe = [k.sb("rQe%d" % h, [64, 128]) for h in range(4)]
                osum = [k.sb("osum%d" % i, [128, 256]) for i in range(nb)]
                tmpb = [k.sb("tmpb%d" % i, [128, 256]) for i in range(nb)]
                ssb = [k.sb("ssb%d" % i, [128, 4]) for i in range(nb)]
                mub = [k.sb("mub%d" % i, [128, 4]) for i in range(nb)]
                outb = [k.sb("outb%d" % i, [128, 256]) for i in range(nb)]
                p_g = k.ps("p_g", [128, 512])
                p_g2 = k.ps("p_g2", [128, 512])
                p_h = [k.ps("p_h%d" % i, [128, 512]) for i in range(4)]
                p_o = k.ps("p_o", [128, 512])
                p_s = k.ps("p_s", [128, 512])
                it = 0
                for d in range(2):
                    c0 = RW_D0 + d * 768
                    for h in range(4):
                        k.memset("pool", Sst[h][:], 0.0, [Sst[h]])
                    for n in tile_order(d):
                        cm_, dd_, gb_, ex_, ops_, ege, os_ = (rot(x, it) for x in (cm, dd, gb, ex, ops, egend, osum))
                        t0 = n * 128
                        k.load(cm_[:], d_rwp[t0:t0 + 128, 0:768], cm_, src=d_rwp)
                        k.load(dd_[:], d_rwp[t0:t0 + 128, c0:c0 + 768], dd_, src=d_rwp)
                        if d == 1:
                            k.load(gb_[:], d_rwp[t0:t0 + 128, RW_G:RW_G + 512], gb_, src=d_rwp)
                        lw = dd_[:, 0:256]
                        k.mm(p_g[:, 0:256], CUM[d][:], lw, [CUM[d], dd_], [p_g])
                        k.mm(p_g[:, 256:512], CUMS[d][:], lw, [CUMS[d], dd_], [p_g])
                        k.mm(p_g2[:, 0:256], OPP_S[d][:], lw, [OPP_S[d], dd_], [p_g2])
                        k.act(ex_[:, 0, :], p_g[:, 0:256], AF.Exp, [p_g], [ex_], scale=-RW_C, partial=True)
                        k.act(ex_[:, 1, :], p_g[:, 0:256], AF.Exp, [p_g], [ex_], scale=RW_C, partial=True)
                        k.act(ex_[:, 2, :], p_g[:, 256:512], AF.Exp, [p_g], [ex_], scale=-RW_C, partial=True)
                        k.act(ex_[:, 3, :], p_g2[:, 0:256], AF.Exp, [p_g2], [ex_], scale=-RW_C, partial=True)
                        kt = dd_[:, 256:512]
                        as_ = dd_[:, 512:768]
                        k.tt("dve", ops_[:, 0, :], as_, ex_[:, 1, :], ALU.mult, [dd_, ex_], [ops_], partial=True)
                        k.tt("pool", ops_[:, 1, :], kt, ex_[:, 1, :], ALU.mult, [dd_, ex_], [ops_], partial=True)
                        k.stt("dve", ops_[:, 2, :], cm_[:, 512:768], -1.0, ex_[:, 2, :], ALU.mult, ALU.mult, [cm_, ex_],
                              [ops_], partial=True)
                        k.tt("pool", ops_[:, 3, :], cm_[:, 0:256], ex_[:, 0, :], ALU.mult, [cm_, ex_], [ops_],
                             partial=True)
                        k.tt("dve", ops_[:, 4, :], as_, ex_[:, 3, :], ALU.mult, [dd_, ex_], [ops_], partial=True)
                        k.tt("pool", ops_[:, 5, :], kt, ex_[:, 3, :], ALU.mult, [dd_, ex_], [ops_], partial=True)
                        for h in range(4):
                            k.mm(p_g2[0:64, 256 + h:257 + h], dd_[:, 64 * h:64 * h + 64], ones[:, 0:1], [dd_, ones],
                                 [p_g2])
                        k.act(ege[:], p_g2[0:64, 256:260], AF.Exp, [p_g2], [ege], scale=-RW_C)
                        if CUT <= 1:
                            it += 1
                            continue
                        for h in range(4):
                            ph = p_h[h]
                            for i in range(4):
                                k.tr(ph[0:64, i * 128:(i + 1) * 128], ops_[:, i, 64 * h:64 * h + 64], ident[:],
                                     [ops_, ident], [ph])
                            k.cp("act", FT[h][:].rearrange("p a b -> p (a b)"), ph[0:64, 0:512], [ph], [FT[h]])
                        if CUT <= 2:
                            it += 1
                            continue
                        for h in range(4):
                            ph = p_h[h]
                            br = FT[h][:, 2:4, :].rearrange("p a b -> p (a b)")
                            k.mm(ph[:, 0:256], FT[h][:, 0, :], br, [FT[h]], [ph])
                            k.mm(ph[:, 256:512], FT[h][:, 1, :], br, [FT[h]], [ph])
                            k.tt("dve", MT[h][:], ph[:].rearrange("p (a b) -> p a b", a=2),
                                 mask2[d][:].unsqueeze(1).to_broadcast([128, 2, 256]), ALU.mult, [ph, mask2[d]],
                                 [MT[h]])
                        if CUT <= 3:
                            it += 1
                            continue
                        for h in range(4):
                            ph = p_h[h]
                            k.mm(ph[:, 0:128], FT[h][:, 2, :], FT[h][:, 0, :], [FT[h]], [ph])
                            k.tt("dve", Nb[h][:], ph[:, 0:128], OPP_S[d][:], ALU.mult, [ph, OPP_S[d]], [Nb[h]])
                            k.cp("pool", NT0[h][:], MT[h][:, 0, 0:128], [MT[h]], [NT0[h]])
                            k.mm(ph[:, 128:192], MT[h][:, 1, 0:128], cm_[:, 256 + 64 * h:320 + 64 * h], [MT[h], cm_],
                                 [ph])
                            k.cp("pool", Xs[h][:, 0:64], ops_[:, 2, 64 * h:64 * h + 64], [ops_], [Xs[h]], partial=True)
                            k.cp("act", Xs[h][:, 64:128], ph[:, 128:192], [ph], [Xs[h]], partial=True)
                        if CUT <= 4:
                            it += 1
                            continue
                        solve_views(Xs, Nb, NT0, N2b, NT2b, p_h)
                        if CUT <= 5:
                            it += 1
                            continue
                        for h in range(4):
                            ph = p_h[h]
                            W = Xs[h][:, 0:64]
                            u0 = Xs[h][:, 64:128]
                            aend = ops_[:, 4, 64 * h:64 * h + 64]
                            kend = ops_[:, 5, 64 * h:64 * h + 64]
                            vh = cm_[:, 256 + 64 * h:320 + 64 * h]
                            attaT = MT[h][:, 0, 128:256]
                            attkT = MT[h][:, 1, 128:256]
                            k.mm(ph[0:64, 0:64], W, aend, [Xs[h], ops_], [ph])
                            k.stt("dve", TTb[h][:], ident[0:64, 0:64], ege[:, h:h + 1], ph[0:64, 0:64], ALU.mult,
                                  ALU.add, [ident, ege, ph], [TTb[h]])
                            k.mm(ph[0:64, 128:256], W, attaT, [Xs[h], MT[h]], [ph])
                            k.tt("dve", Qe[h][:], FT[h][:, 3, :], ph[0:64, 128:256], ALU.add, [FT[h], ph], [Qe[h]])
                            k.mm(p_o[:, 64 * h:64 * h + 64], Qe[h][:], Sst[h][:], [Qe[h], Sst[h]], [p_o], start=True,
                                 stop=False)
                            k.mm(p_o[:, 64 * h:64 * h + 64], attaT, u0, [MT[h], Xs[h]], [p_o], start=False, stop=False)
                            k.mm(p_o[:, 64 * h:64 * h + 64], attkT, vh, [MT[h], cm_], [p_o], start=False, stop=True)
                            k.mm(p_s[0:64, 64 * h:64 * h + 64], TTb[h][:], Sst[h][:], [TTb[h], Sst[h]], [p_s],
                                 start=True, stop=False)
                            k.mm(p_s[0:64, 64 * h:64 * h + 64], aend, u0, [ops_, Xs[h]], [p_s], start=False, stop=False)
                            k.mm(p_s[0:64, 64 * h:64 * h + 64], kend, vh, [ops_, cm_], [p_s], start=False, stop=True)
                            k.cp("act", Sst[h][:], p_s[0:64, 64 * h:64 * h + 64], [p_s], [Sst[h]])
                        if d == 0:
                            k.cp("act", hsum[:, n, :], p_o[:, 0:256], [p_o], [hsum], partial=True)
                        else:
                            tb, sb_, mb_, ob_ = rot(tmpb, it), rot(ssb, it), rot(mub, it), rot(outb, it)
                            k.tt("dve", os_[:], p_o[:, 0:256], hsum[:, n, :], ALU.add, [p_o, hsum], [os_])
                            os3 = os_[:].rearrange("p (h d) -> p h d", h=4)
                            k.red("dve", mb_[:], os3, [os_], [mb_])
                            k.ts("dve", mb_[:], mb_[:], 1.0 / 64, ALU.mult, [mb_], [mb_])
                            k.tt("dve", os3, os3, mb_[:].unsqueeze(2).to_broadcast([128, 4, 64]), ALU.subtract,
                                 [os_, mb_], [os_])
                            k.tt("pool", tb[:], os_[:], os_[:], ALU.mult, [os_], [tb])
                            k.red("dve", sb_[:], tb[:].rearrange("p (h d) -> p h d", h=4), [tb], [sb_])
                            k.rsqrt(sb_[:], sb_[:], [sb_], [sb_], scale=1.0 / 64, eps=GN_EPS)
                            k.tt("dve", ob_[:].rearrange("p (h d) -> p h d", h=4), os3,
                                 sb_[:].unsqueeze(2).to_broadcast([128, 4, 64]), ALU.mult, [os_, sb_], [ob_])
                            k.tt("pool", ob_[:], ob_[:], gnw[:], ALU.mult, [ob_, gnw], [ob_])
                            k.tt("pool", ob_[:], ob_[:], gnb[:], ALU.add, [ob_, gnb], [ob_])
                            k.tt("dve", ob_[:], ob_[:], gb_[:, 256:512], ALU.add, [ob_, gb_], [ob_])
                            k.tt("dve", ob_[:], ob_[:], gb_[:, 0:256], ALU.mult, [ob_, gb_], [ob_])
                            k.store(d_mixed[t0:t0 + 128, 256:512], ob_[:], ob_, d_mixed)
                        it += 1
                S.end_phase()

        if on("init"):
            phase_init_xs()
        for l in layers:
            last = (l == DEPTH - 1)
            if on("mod"):
                phase_mod(l)
            if on("A"):
                phase_A(l)
            if on("mlstm"):
                phase_mlstm(l)
            if on("rwkv_pre"):
                phase_rwkv_pre(l)
            if on("rwkv"):
                phase_rwkv(l)
            if on("gla"):
                phase_gla(l)
            if on("gdn_conv"):
                phase_gdn_conv(l)
            if on("gdn"):
                phase_gdn(l)
            if on("B0"):
                phase_B0(l, last)
            if on("B1"):
                phase_B1(l, last)
        build.ninstr = S.ninstr
    return nc


PARAM_RESHAPE = {"ml_ig_b": (DEPTH, 8), "ml_fg_b": (DEPTH, 8), "gd_conv_w": (DEPTH, 9, 768),
                 "gd_a_log": (DEPTH, 8), "gd_dt_bias": (DEPTH, 8), "final_norm_w": (1, D)}
PARAM_NAMES = ["ada_w", "ada_b", "norm1_w", "norm2_w", "w_in", "w_out", "ml_ig_b", "ml_fg_b", "ml_norm_w",
               "rw_mu_prev", "rw_mu_next", "rw_w0", "rw_w_up", "rw_a0", "rw_a_up", "rw_g_up", "rw_k_k", "rw_k_a",
               "rw_r_k", "rw_gn_w", "rw_gn_b", "gl_gate_up", "gl_gate_b", "gl_norm_w", "gd_conv_w", "gd_a_log",
               "gd_dt_bias", "gd_norm_w", "mlp_w1", "mlp_w2", "final_norm_w"]


def prep_params(inputs):
    out = {}
    for nm in PARAM_NAMES:
        a = np.ascontiguousarray(np.asarray(inputs[nm], dtype=np.float32))
        if nm in PARAM_RESHAPE:
            a = a.reshape(PARAM_RESHAPE[nm])
        out[nm] = a
    return out


_NC_CACHE = {}


def kernel(**inputs):
    x = np.asarray(inputs["x"], dtype=np.float32)
    c = np.asarray(inputs["c"], dtype=np.float32)
    ctx = np.asarray(inputs["ctx"], dtype=np.float32)
    c_ctx = np.asarray(inputs["c_ctx"], dtype=np.float32).reshape(1, D)
    B = x.shape[0]
    params = prep_params(inputs)
    if "full" not in _NC_CACHE:
        _NC_CACHE["full"] = build({})
    nc = _NC_CACHE["full"]
    in_maps = []
    for b in range(B):
        m = dict(params)
        m["x"] = np.ascontiguousarray(x[b])
        m["c"] = np.ascontiguousarray(c[b:b + 1])
        m["ctx"] = np.ascontiguousarray(ctx[b])
        m["c_ctx"] = c_ctx
        in_maps.append(m)
    res = run_bass_kernel_spmd(nc, in_maps, core_ids=list(range(B)))
    return np.stack([np.asarray(r["out"], dtype=np.float32) for r in res.results], axis=0)
```
